# Optimizing a Trainium2 kernel written in Bass

```python
import math
import jax, jax.numpy as jnp
from jax import lax
import numpy as np

D_MODEL = 1024
BATCH = 2
SEQ = 16384
DEPTH = 2

FOX_HEADS = 8
FOX_HEAD_DIM = 64
DIFF_HEADS = 4
DIFF_HEAD_DIM = 64
DIFF_V_DIM = 2 * DIFF_HEAD_DIM
FOX_WIDTH = FOX_HEADS * FOX_HEAD_DIM
DIFF_WIDTH = DIFF_HEADS * DIFF_V_DIM
MIX_WIDTH = FOX_WIDTH + DIFF_WIDTH
D_FF = 2816
Q_BLOCK = 128
ROPE_THETA = 10000.0
NORM_EPS = 1e-5
FFN_RES_WEIGHT = 0.5
IN_SPLITS = [FOX_WIDTH, FOX_WIDTH, FOX_WIDTH, FOX_HEADS,
             2 * DIFF_HEADS * DIFF_HEAD_DIM, 2 * DIFF_HEADS * DIFF_HEAD_DIM, DIFF_WIDTH]
IN_COLS = int(sum(IN_SPLITS))
IN_OFFSETS = [int(v) for v in np.cumsum(IN_SPLITS)[:-1]]

kernel_name = "hymba_fox_diffattn_macaron"


def _rmsnorm(x, gain):
    x32 = x.astype(jnp.float32)
    y = x32 * lax.rsqrt(jnp.mean(x32 * x32, axis=-1, keepdims=True) + NORM_EPS)
    return (y * gain.astype(jnp.float32)).astype(x.dtype)


def _swiglu(h, w_gate_up, w_down):
    g, u = jnp.split(h @ w_gate_up, 2, axis=-1)
    return (jax.nn.silu(g) * u) @ w_down


def _rope_tables(seq, dim):
    inv_freq = 1.0 / (ROPE_THETA ** (jnp.arange(0, dim, 2, dtype=jnp.float32) / dim))
    pos = jnp.arange(seq, dtype=jnp.float32)
    freqs = pos[:, None] * inv_freq[None, :]
    emb = jnp.concatenate([freqs, freqs], axis=-1)
    return jnp.cos(emb), jnp.sin(emb)


def _apply_rope(t, cos, sin):
    t32 = t.astype(jnp.float32)
    t1, t2 = jnp.split(t32, 2, axis=-1)
    rot = jnp.concatenate([-t2, t1], axis=-1)
    return (t32 * cos + rot * sin).astype(t.dtype)


def _causal_block_attention(q_f, k_f, v_f, cum_f, q_d, k_d, v_d, lam):
    B, Hf, S, Df = q_f.shape
    Hd, Dh = q_d.shape[1], q_d.shape[-1]
    Dv = v_d.shape[-1]
    nb = S // Q_BLOCK
    qf_blk = jnp.moveaxis(q_f.reshape(B, Hf, nb, Q_BLOCK, Df), 2, 0)
    cf_blk = jnp.moveaxis(cum_f.reshape(B, Hf, nb, Q_BLOCK), 2, 0)
    qd_blk = jnp.moveaxis(q_d.reshape(B, Hd, 2, nb, Q_BLOCK, Dh), 3, 0)
    k_pos = jnp.arange(S)
    scale_f = Df ** -0.5
    scale_d = Dh ** -0.5

    def one_block(args):
        qf, cf, qd, blk = args
        q_pos = blk * Q_BLOCK + jnp.arange(Q_BLOCK)
        causal = k_pos[None, :] <= q_pos[:, None]
        s_f = jnp.einsum('bhqd,bhkd->bhqk', qf, k_f).astype(jnp.float32) * scale_f
        s_f = s_f + cf[..., :, None] - cum_f[..., None, :]
        s_f = jnp.where(causal, s_f, -jnp.inf)
        p_f = jax.nn.softmax(s_f, axis=-1)
        o_f = jnp.einsum('bhqk,bhkd->bhqd', p_f.astype(v_f.dtype), v_f)
        s_d = jnp.einsum('bhcqd,bhckd->bhcqk', qd, k_d).astype(jnp.float32) * scale_d
        s_d = jnp.where(causal, s_d, -jnp.inf)
        p_d = jax.nn.softmax(s_d, axis=-1)
        a_d = p_d[:, :, 0] - lam * p_d[:, :, 1]
        o_d = jnp.einsum('bhqk,bhkd->bhqd', a_d.astype(v_d.dtype), v_d)
        return o_f, o_d

    o_f, o_d = lax.map(one_block, (qf_blk, cf_blk, qd_blk, jnp.arange(nb)))
    o_f = jnp.moveaxis(o_f, 0, 2).reshape(B, Hf, S, Df)
    o_d = jnp.moveaxis(o_d, 0, 2).reshape(B, Hd, S, Dv)
    return o_f, o_d


def _hybrid_mixer(h, w_in, forget_bias, lq1, lk1, lq2, lk2, subln, w_out, lambda_init):
    B, S, _ = h.shape
    proj = h @ w_in
    fq, fk, fv, ff, dq, dk, dv = jnp.split(proj, IN_OFFSETS, axis=-1)
    to_heads_f = lambda t: t.reshape(B, S, FOX_HEADS, FOX_HEAD_DIM).transpose(0, 2, 1, 3)
    q_f, k_f, v_f = to_heads_f(fq), to_heads_f(fk), to_heads_f(fv)
    log_f = jax.nn.log_sigmoid((ff + forget_bias).astype(jnp.float32))
    cum_f = jnp.cumsum(log_f.transpose(0, 2, 1), axis=-1)
    to_heads_d = lambda t: t.reshape(B, S, DIFF_HEADS, 2, DIFF_HEAD_DIM).transpose(0, 2, 3, 1, 4)
    cos, sin = _rope_tables(S, DIFF_HEAD_DIM)
    q_d = _apply_rope(to_heads_d(dq), cos, sin)
    k_d = _apply_rope(to_heads_d(dk), cos, sin)
    v_d = dv.reshape(B, S, DIFF_HEADS, DIFF_V_DIM).transpose(0, 2, 1, 3)
    lam = (jnp.exp(jnp.sum(lq1.astype(jnp.float32) * lk1.astype(jnp.float32)))
           - jnp.exp(jnp.sum(lq2.astype(jnp.float32) * lk2.astype(jnp.float32)))
           + lambda_init)
    o_f, o_d = _causal_block_attention(q_f, k_f, v_f, cum_f, q_d, k_d, v_d, lam)
    o_d = _rmsnorm(o_d, subln) * (1.0 - lambda_init)
    o_f = o_f.transpose(0, 2, 1, 3).reshape(B, S, FOX_WIDTH)
    o_d = o_d.transpose(0, 2, 1, 3).reshape(B, S, DIFF_WIDTH)
    return jnp.concatenate([o_f, o_d], axis=-1) @ w_out


def setup_inputs(seed: int = 0) -> dict:
    key = jax.random.key(seed)
    ks = jax.random.split(key, 20)
    nrm = lambda k, shape, fan_in: jax.random.normal(k, shape, jnp.float32) * fan_in ** -0.5
    gain = lambda k, shape: 1.0 + 0.02 * jax.random.normal(k, shape, jnp.float32)
    return {
        "x": jax.random.normal(ks[0], (BATCH, SEQ, D_MODEL), jnp.float32),
        "ffn1_norm": gain(ks[1], (DEPTH, D_MODEL)),
        "ffn1_w_gate_up": nrm(ks[2], (DEPTH, D_MODEL, 2 * D_FF), D_MODEL),
        "ffn1_w_down": nrm(ks[3], (DEPTH, D_FF, D_MODEL), D_FF),
        "mix_norm": gain(ks[4], (DEPTH, D_MODEL)),
        "w_in": nrm(ks[5], (DEPTH, D_MODEL, IN_COLS), D_MODEL),
        "forget_bias": jax.random.uniform(ks[6], (DEPTH, FOX_HEADS), jnp.float32, 1.0, 4.0),
        "lambda_q1": 0.1 * jax.random.normal(ks[7], (DEPTH, DIFF_HEAD_DIM), jnp.float32),
        "lambda_k1": 0.1 * jax.random.normal(ks[8], (DEPTH, DIFF_HEAD_DIM), jnp.float32),
        "lambda_q2": 0.1 * jax.random.normal(ks[9], (DEPTH, DIFF_HEAD_DIM), jnp.float32),
        "lambda_k2": 0.1 * jax.random.normal(ks[10], (DEPTH, DIFF_HEAD_DIM), jnp.float32),
        "diff_subln": gain(ks[11], (DEPTH, DIFF_V_DIM)),
        "w_out": nrm(ks[12], (DEPTH, MIX_WIDTH, D_MODEL), MIX_WIDTH),
        "ffn2_norm": gain(ks[13], (DEPTH, D_MODEL)),
        "ffn2_w_gate_up": nrm(ks[14], (DEPTH, D_MODEL, 2 * D_FF), D_MODEL),
        "ffn2_w_down": nrm(ks[15], (DEPTH, D_FF, D_MODEL), D_FF),
        "final_norm": gain(ks[16], (D_MODEL,)),
    }


def reference(x, ffn1_norm, ffn1_w_gate_up, ffn1_w_down, mix_norm, w_in, forget_bias,
              lambda_q1, lambda_k1, lambda_q2, lambda_k2, diff_subln, w_out,
              ffn2_norm, ffn2_w_gate_up, ffn2_w_down, final_norm):
    for layer in range(DEPTH):
        lambda_init = 0.8 - 0.6 * math.exp(-0.3 * layer)
        x = x + FFN_RES_WEIGHT * _swiglu(_rmsnorm(x, ffn1_norm[layer]),
                                         ffn1_w_gate_up[layer], ffn1_w_down[layer])
        x = x + _hybrid_mixer(_rmsnorm(x, mix_norm[layer]), w_in[layer], forget_bias[layer],
                              lambda_q1[layer], lambda_k1[layer], lambda_q2[layer],
                              lambda_k2[layer], diff_subln[layer], w_out[layer], lambda_init)
        x = x + FFN_RES_WEIGHT * _swiglu(_rmsnorm(x, ffn2_norm[layer]),
                                         ffn2_w_gate_up[layer], ffn2_w_down[layer])
    return _rmsnorm(x, final_norm)
```

```python
import math
import numpy as np
import ml_dtypes
import concourse.bass as bass
import concourse.mybir as mybir
from concourse.bass_utils import run_bass_kernel_spmd

F32 = mybir.dt.float32
BF16 = mybir.dt.bfloat16
AF = mybir.ActivationFunctionType
ALU = mybir.AluOpType
AX = mybir.AxisListType

D = 1024
DFF = 2816
NCH = 22
NLOC = 4096
NT = 32
TB = 1024
NTB = NLOC // TB
EPS = 1e-5
INC = 4104
OFF_FQ, OFF_FK, OFF_FV, OFF_FF, OFF_DQ, OFF_DK, OFF_DV, OFF_DQS, OFF_DKS = 0, 512, 1024, 1536, 1544, 2056, 2568, 3080, 3592
NEG = -30000.0
SKIP = set()


class Sem:
    def __init__(self, h):
        self.h = h
        self.cnt = 0


class Buf:
    __slots__ = ("w", "r")

    def __init__(self):
        self.w = None
        self.r = []


class Prog:
    ENG = ("pe", "act", "dve", "pool", "sp")

    def __init__(self, nc, sems):
        self.nc = nc
        self.es = {e: Sem(sems[i]) for i, e in enumerate(self.ENG)}
        n = len(self.ENG)
        self.dpool = {"pool": [Sem(h) for h in sems[n:n + 12]], "sp": [Sem(h) for h in sems[n + 12:n + 36]],
                      "act": [Sem(h) for h in sems[n + 36:n + 40]]}
        self.di = {"pool": 0, "sp": 0, "act": 0}
        self.ops = {e: [] for e in self.ENG}
        self.waited = {e: {} for e in self.ENG}
        self.last_tok = {}

    def _deps(self, reads, writes, extra):
        w = []
        for b in reads:
            if b.w is not None:
                w.append(b.w)
        for b in writes:
            w.extend(b.r)
            if b.w is not None:
                w.append(b.w)
        w.extend(t for t in extra if t is not None)
        return w

    def _upd(self, tok, reads, writes):
        for b in reads:
            b.r.append(tok)
        for b in writes:
            b.w = tok
            b.r = []

    def op(self, eng, fn, reads=(), writes=(), extra=(), noembed=False):
        if noembed:
            fn.noembed = True
        s = self.es[eng]
        waits = self._deps(reads, writes, extra)
        s.cnt += 1
        tok = (s, s.cnt)
        self.ops[eng].append((fn, waits, s, 1))
        self._upd(tok, reads, writes)
        return tok

    def dma(self, eng, out, in_, reads=(), writes=(), extra=(), **kw):
        pool = self.dpool[eng]
        ds = pool[self.di[eng] % len(pool)]
        self.di[eng] += 1
        waits = self._deps(reads, writes, extra)
        if ds.cnt > 0:
            waits.append((ds, 16 * ds.cnt))
        ds.cnt += 1
        tok = (ds, 16 * ds.cnt)

        def fn(e, out=out, in_=in_, kw=kw):
            return e.dma_start(out=out, in_=in_, **kw)
        self.ops[eng].append((fn, waits, ds, 16))
        self._upd(tok, reads, writes)
        return tok

    def raw(self, eng, fn, extra=()):
        self.ops[eng].append((fn, [t for t in extra if t is not None], None, 0))

    def flush(self):
        nc = self.nc
        engs = {"pe": "tensor", "act": "scalar", "dve": "vector", "pool": "gpsimd", "sp": "sync"}
        all_sems = list(self.es.values()) + [d for p in self.dpool.values() for d in p]
        finals = [(s, s.cnt if s in self.es.values() else 16 * s.cnt) for s in all_sems]
        with nc.Block() as block:
            for ename in self.ENG:
                def body(e, ename=ename):
                    wd = self.waited[ename]
                    mine = self.es[ename]
                    for fn, waits, sem, inc in self.ops[ename]:
                        need = {}
                        for (s, v) in waits:
                            if s is mine and ename == "pe":
                                continue
                            if wd.get(s, 0) >= v:
                                continue
                            if need.get(s, 0) < v:
                                need[s] = v
                        items = list(need.items())
                        embed = None
                        if items and inc == 1 and ename in ("act", "dve", "pool") and not getattr(fn, "noembed", False):
                            embed = items.pop()
                        for s, v in items:
                            e.wait_ge(s.h, v)
                            wd[s] = v
                        ins = fn(e)
                        if embed is not None:
                            ins._wait_ge(embed[0].h, embed[1])
                            wd[embed[0]] = embed[1]
                        if sem is not None:
                            ins.then_inc(sem.h, inc)
                    for s, v in finals:
                        if v > 0 and wd.get(s, 0) < v:
                            e.wait_ge(s.h, v)
                            wd[s] = v
                getattr(block, engs[ename])(body)
        self.ops = {e: [] for e in self.ENG}


def build(debug=False, stop=99):
    nc = bass.Bass("TRN2", target_bir_lowering=False)
    dt = nc.dram_tensor
    x_in = dt("x", [NLOC, D], F32, kind="ExternalInput")
    wgu = [dt("wgu1", [2, D, 2 * DFF], F32, kind="ExternalInput"), dt("wgu2", [2, D, 2 * DFF], F32, kind="ExternalInput")]
    wdn = [dt("wd1", [2, DFF, D], F32, kind="ExternalInput"), dt("wd2", [2, DFF, D], F32, kind="ExternalInput")]
    win = dt("win", [2, D, INC], F32, kind="ExternalInput")
    wout = dt("wout", [2, D, D], F32, kind="ExternalInput")
    g_ffn1 = dt("g_ffn1", [2, D], F32, kind="ExternalInput")
    g_mix = dt("g_mix", [2, D], F32, kind="ExternalInput")
    g_ffn2 = dt("g_ffn2", [2, D], F32, kind="ExternalInput")
    g_fin = dt("g_fin", [1, D], F32, kind="ExternalInput")
    fbias = dt("fbias", [2, 8], F32, kind="ExternalInput")
    lam_in = dt("lam_in", [2, 4, 64], F32, kind="ExternalInput")
    subln = dt("subln", [2, 128], F32, kind="ExternalInput")
    cosT = dt("cosT", [128, NLOC], F32, kind="ExternalInput")
    sinT = dt("sinT", [128, NLOC], F32, kind="ExternalInput")
    maskd = dt("maskd", [128, 512], F32, kind="ExternalInput")
    onehot = dt("onehot", [8, 4], F32, kind="ExternalInput")
    identd = dt("identd", [128, 128], BF16, kind="ExternalInput")
    triud = dt("triud", [128, 128], F32, kind="ExternalInput")
    y_out = dt("y", [NLOC, D], F32, kind="ExternalOutput")

    X1 = dt("X1", [NLOC, D], F32)
    OC = dt("OC", [NLOC, D], BF16)
    QF = dt("QF", [8, 70, NLOC], BF16)
    QD = dt("QD", [4, 128, NLOC], BF16)
    KF = [[dt(f"KF{l}_{h}", [70, NLOC], BF16) for h in range(8)] for l in range(2)]
    VF = [[dt(f"VF{l}_{h}", [NLOC, 64], BF16) for h in range(8)] for l in range(2)]
    KD = [[dt(f"KD{l}_{h}", [128, NLOC], BF16) for h in range(4)] for l in range(2)]
    VD = [[dt(f"VD{l}_{h}", [NLOC, 128], BF16) for h in range(4)] for l in range(2)]
    TTI = [dt(f"TTI{l}", [8, NT], F32) for l in range(2)]
    KFg = [[dt(f"KFg{l}_{h}", [4, 70, NLOC], BF16) for h in range(8)] for l in range(2)]
    VFg = [[dt(f"VFg{l}_{h}", [4, NLOC, 64], BF16) for h in range(8)] for l in range(2)]
    KDg = [[dt(f"KDg{l}_{h}", [4, 128, NLOC], BF16) for h in range(4)] for l in range(2)]
    VDg = [[dt(f"VDg{l}_{h}", [4, NLOC, 128], BF16) for h in range(4)] for l in range(2)]
    TTg = [dt(f"TTg{l}", [4, 8, NT], F32) for l in range(2)]
    dbg = {}
    if debug:
        dbg["QF"] = dt("dbgQF", [8, 70, NLOC], BF16, kind="ExternalOutput")
        dbg["QD"] = dt("dbgQD", [4, 128, NLOC], BF16, kind="ExternalOutput")
        dbg["X1"] = dt("dbgX1", [NLOC, D], F32, kind="ExternalOutput")
        dbg["OC"] = dt("dbgOC", [NLOC, D], BF16, kind="ExternalOutput")
        dbg["KF"] = dt("dbgKF", [8, 70, NLOC], BF16, kind="ExternalOutput")
        dbg["VF"] = dt("dbgVF", [8, NLOC, 64], BF16, kind="ExternalOutput")
        dbg["KD"] = dt("dbgKD", [4, 128, NLOC], BF16, kind="ExternalOutput")
        dbg["VD"] = dt("dbgVD", [4, NLOC, 128], BF16, kind="ExternalOutput")
        dbg["KFg"] = dt("dbgKFg", [4, 70, NLOC], BF16, kind="ExternalOutput")
        dbg["VDg"] = dt("dbgVDg", [4, NLOC, 128], BF16, kind="ExternalOutput")

    groups = [[0, 1, 2, 3], [4, 5, 6, 7]]
    _u = [0]

    def uniq(name):
        _u[0] += 1
        return f"{name}_{_u[0]}"
    NSEM = 5 + 40 + 12
    from contextlib import ExitStack
    with ExitStack() as gs:
        sems = [gs.enter_context(nc.semaphore(f"s{i}")) for i in range(NSEM)]
        P = Prog(nc, sems)
        ccsem = Sem(sems[45])

        def collective(inp, outp, extra):
            ccsem.cnt += 1

            def fn(e, inp=inp, outp=outp):
                e.collective_compute("AllGather", ALU.bypass, replica_groups=groups,
                                     ins=[inp.ap().opt()], outs=[outp.ap().opt()]).then_inc(ccsem.h, 1)
                return None
            P.raw("pool", fn, extra=extra)
            return (ccsem, ccsem.cnt)

        ident = gs.enter_context(nc.sbuf_tensor("ident", [128, 128], BF16))
        zer_bf = gs.enter_context(nc.sbuf_tensor("zer_bf", [128, 512], BF16))
        wT = gs.enter_context(nc.sbuf_tensor("wT", [8, NLOC], F32))
        epsb = gs.enter_context(nc.sbuf_tensor("epsb", [128, 1], F32))
        oneb = gs.enter_context(nc.sbuf_tensor("oneb", [128, 1], F32))
        B_ident, B_ones, B_zer, B_wT = Buf(), Buf(), Buf(), Buf()
        with nc.sbuf_tensor("ones_bf", [8, 3, NLOC], BF16) as ones_bf:
            P.dma("sp", ident[:], identd.ap()[:, :], writes=[B_ident])
            P.op("dve", lambda e: e.memset(ones_bf[:], 1.0), writes=[B_ones])
            P.op("dve", lambda e: e.memset(zer_bf[:], 0.0), writes=[B_zer])
            P.op("dve", lambda e: e.memset(epsb[:], EPS))
            P.op("dve", lambda e: e.memset(oneb[:], 1.0))
            P.dma("sp", QF.ap()[:, 67:70, :], ones_bf[:], reads=[B_ones])
            for l in range(2):
                for h in range(8):
                    P.dma("sp", KF[l][h].ap()[64:67, :], ones_bf[0:3, 0, :], reads=[B_ones])
            P.flush()

        def phase_tokens(layer, first, last):
            with ExitStack() as ps:
                sb = lambda name, shape, dtp: ps.enter_context(nc.sbuf_tensor(uniq(name), shape, dtp))
                xs = sb("xs", [128, 8, D], F32)
                hT = sb("hT", [128, 8, TB], BF16)
                AT = sb("AT", [128, NCH, TB], BF16)
                RING = 6
                ring = sb("ring", [128, RING, 8, 512], BF16)
                gain = sb("gain", [128, 2, D], F32)
                hn = sb("hn", [128, 2, D], BF16)
                junk = sb("junk", [128, D], BF16)
                ssq = sb("ssq", [128, 8], F32)
                rms = sb("rms", [128, 8], F32)
                rstd = sb("rstd", [128, 8], F32)
                sg = sb("sg", [128, 2, 512], F32)
                ocl = sb("ocl", [128, 2, D], BF16) if not first else None
                trig = sb("trig", [128, 2, 2, 512], F32) if not last else None
                rt = sb("rt", [128, 3, 512], F32) if not last else None
                stg = sb("stg", [128, 3, 512], BF16) if not last else None
                vft = sb("vft", [128, 2, 512], BF16) if not last else None
                vdt = sb("vdt", [128, 2, 512], BF16) if not last else None
                fbt = sb("fbt", [128, 8], F32)
                zz = sb("zz", [128, 8, 8], F32)
                za = sb("za", [128, 8, 8], F32)
                ze = sb("ze", [128, 8, 8], F32)
                lf = sb("lf", [128, 8, 8], F32)
                triu = sb("triu", [128, 128], F32)
                yt = sb("yt", [128, 2, D], F32) if last else None
                tp = ps.enter_context(nc.psum_tensor(uniq("tp"), [128, 2, 8, 128], BF16))
                acc = ps.enter_context(nc.psum_tensor(uniq("acc"), [128, 6, 512], F32))

                B_xs = [Buf() for _ in range(8)]
                B_hT = [Buf() for _ in range(8)]
                B_AT = [[Buf(), Buf()] for _ in range(NCH)]
                B_ring = [[Buf(), Buf()] for _ in range(RING)]
                B_gain = [Buf(), Buf()]
                B_hn = [Buf(), Buf()]
                B_junk, B_ssq, B_rms, B_rstd = Buf(), Buf(), Buf(), Buf()
                B_sg = [Buf(), Buf()]
                B_ocl = [Buf(), Buf()]
                B_trig = [Buf(), Buf()]
                B_rt = [Buf(), Buf(), Buf()]
                B_stg = [Buf(), Buf(), Buf()]
                B_vft = [Buf(), Buf()]
                B_vdt = [Buf(), Buf()]
                B_fbt, B_zz, B_za, B_ze, B_lf, B_triu = Buf(), Buf(), Buf(), Buf(), Buf(), Buf()
                B_yt = [Buf(), Buf()]
                B_tp = [Buf(), Buf()]
                B_acc = [Buf() for _ in range(6)]
                st = {"ring": 0, "acc": 0, "gain": 0, "hn": 0, "sg": 0, "stg": 0, "rt": 0}

                def nxt(k, n):
                    v = st[k] % n
                    st[k] += 1
                    return v

                def ring_tile(parts):
                    s = nxt("ring", RING)
                    for (c0, c1, src, nj) in parts:
                        bufs = []
                        if c0 < 256:
                            bufs.append(B_ring[s][0])
                        if c1 > 256:
                            bufs.append(B_ring[s][1])
                        P.dma("pool", ring[:, s, 0:nj, c0:c1], src, writes=bufs)
                    return s

                def wsrc(wh, l, r0, nj, c0, c1):
                    return wh[l].rearrange("(j p) n -> p j n", p=128)[:, r0:r0 + nj, c0:c1]

                P.dma("sp", triu[:], triud[:, :], writes=[B_triu])

                def load_gain(src_ap):
                    g = nxt("gain", 2)
                    P.dma("sp", gain[:, g], src_ap.partition_broadcast(128), writes=[B_gain[g]])
                    return g

                def norm_transpose(g):
                    for t in range(8):
                        P.op("act", lambda e, t=t: e.activation(out=junk[:], in_=xs[:, t], func=AF.Square,
                                                                accum_out=ssq[:, t:t + 1]),
                             reads=[B_xs[t]], writes=[B_junk, B_ssq], noembed=True)
                    P.op("act", lambda e: e.activation(out=rms[:], in_=ssq[:], func=AF.Sqrt, bias=epsb[:, 0:1], scale=1.0 / D),
                         reads=[B_ssq], writes=[B_rms])
                    P.op("dve", lambda e: e.reciprocal(out=rstd[:], in_=rms[:]), reads=[B_rms], writes=[B_rstd])
                    for t in range(8):
                        k = nxt("hn", 2)
                        P.op("dve", lambda e, t=t, k=k: e.scalar_tensor_tensor(
                            out=hn[:, k], in0=xs[:, t], scalar=rstd[:, t:t + 1], in1=gain[:, g],
                            op0=ALU.mult, op1=ALU.mult),
                            reads=[B_xs[t], B_rstd, B_gain[g]], writes=[B_hn[k]])
                        transpose_to_hT(hn, k, B_hn[k], t)

                def transpose_to_hT(src, k, bsrc, t):
                    def tfn(e, k=k, src=src):
                        ins = None
                        for j in range(8):
                            ins = e.transpose(out=tp[:, k, j, :], in_=src[:, k, j * 128:(j + 1) * 128], identity=ident[:])
                        return ins
                    P.op("pe", tfn, reads=[bsrc, B_ident], writes=[B_tp[k]])
                    P.op("act", lambda e, k=k, t=t: e.activation(out=hT[:, :, t * 128:(t + 1) * 128], in_=tp[:, k], func=AF.Copy),
                         reads=[B_tp[k]], writes=[B_hT[t]])

                def mm_group(out_ap, pairs, reads, bacc):
                    def fn(e, out_ap=out_ap, pairs=pairs):
                        ins = None
                        n = len(pairs)
                        for i, (l_, r_) in enumerate(pairs):
                            ins = e.matmul(out_ap, lhsT=l_, rhs=r_, start=(i == 0), stop=(i == n - 1))
                        return ins
                    return P.op("pe", fn, reads=reads, writes=[bacc])

                def ffn(l, which):
                    g = load_gain((g_ffn1 if which == 0 else g_ffn2).ap()[l])
                    norm_transpose(g)
                    W = wgu[which]

                    def gu_tile(ti):
                        return ring_tile([(0, 256, wsrc(W, l, 0, 8, 256 * ti, 256 * ti + 256), 8),
                                          (256, 512, wsrc(W, l, 0, 8, DFF + 256 * ti, DFF + 256 * ti + 256), 8)])

                    def gu_work(ti, s, half):
                        for cc in range(2):
                            c = 2 * ti + cc
                            hb = [B_hT[4 * half + q] for q in range(4)]
                            bg = nxt("acc", 6)
                            mm_group(acc[:, bg], [(ring[:, s, j, cc * 128:(cc + 1) * 128], hT[:, j, half * 512:(half + 1) * 512]) for j in range(8)],
                                     hb + [B_ring[s][0]], B_acc[bg])
                            bu = nxt("acc", 6)
                            mm_group(acc[:, bu], [(ring[:, s, j, 256 + cc * 128:256 + (cc + 1) * 128], hT[:, j, half * 512:(half + 1) * 512]) for j in range(8)],
                                     hb + [B_ring[s][1]], B_acc[bu])
                            k = nxt("sg", 2)
                            P.op("act", lambda e, k=k, bg=bg: e.activation(out=sg[:, k], in_=acc[:, bg], func=AF.Silu),
                                 reads=[B_acc[bg]], writes=[B_sg[k]])
                            P.op("dve", lambda e, k=k, bu=bu, c=c, half=half: e.tensor_tensor(
                                out=AT[:, c, half * 512:(half + 1) * 512], in0=acc[:, bu], in1=sg[:, k], op=ALU.mult),
                                reads=[B_acc[bu], B_sg[k]], writes=[B_AT[c][half]])

                    s0 = gu_tile(0)
                    s1 = gu_tile(1)
                    gu_work(0, s0, 0)
                    gu_work(1, s1, 0)
                    gu_work(0, s0, 1)
                    gu_work(1, s1, 1)
                    for ti in range(2, NCH // 2):
                        s = gu_tile(ti)
                        for half in range(2):
                            gu_work(ti, s, half)
                    Wd = wdn[which]
                    for nh in range(2):
                        slots = []
                        for cg in range(3):
                            nj = 8 if cg < 2 else 6
                            slots.append(ring_tile([(0, 512, wsrc(Wd, l, 8 * cg, nj, nh * 512, nh * 512 + 512), nj)]))
                        for t in range(8):
                            b = nxt("acc", 6)
                            mm_group(acc[:, b], [(AT[:, c, t * 128:(t + 1) * 128], ring[:, slots[c // 8], c % 8, :]) for c in range(NCH)],
                                     [B_AT[c][t // 4] for c in range(NCH)] + [B_ring[s_][q] for s_ in slots for q in range(2)], B_acc[b])
                            P.op("dve", lambda e, b=b, t=t, nh=nh: e.scalar_tensor_tensor(
                                out=xs[:, t, nh * 512:(nh + 1) * 512], in0=acc[:, b], scalar=0.5, in1=xs[:, t, nh * 512:(nh + 1) * 512],
                                op0=ALU.mult, op1=ALU.add),
                                reads=[B_acc[b]], writes=[B_xs[t]])

                def outproj(l, tb):
                    for t in range(8):
                        k = t % 2
                        row0 = tb * TB + t * 128
                        P.dma("sp", ocl[:, k], OC.ap()[row0:row0 + 128, :], writes=[B_ocl[k]])
                        transpose_to_hT(ocl, k, B_ocl[k], t)
                    slots = [ring_tile([(0, 512, wsrc(wout, l, 0, 8, nh * 512, nh * 512 + 512), 8)]) for nh in range(2)]
                    for nh in range(2):
                        s = slots[nh]
                        for t in range(8):
                            b = nxt("acc", 6)
                            mm_group(acc[:, b], [(hT[:, j, t * 128:(t + 1) * 128], ring[:, s, j, :]) for j in range(8)],
                                     [B_hT[t], B_ring[s][0], B_ring[s][1]], B_acc[b])
                            P.op("dve", lambda e, b=b, t=t, nh=nh: e.tensor_tensor(
                                out=xs[:, t, nh * 512:(nh + 1) * 512], in0=acc[:, b], in1=xs[:, t, nh * 512:(nh + 1) * 512], op=ALU.add),
                                reads=[B_acc[b]], writes=[B_xs[t]])

                def proj(l, tb):
                    g = load_gain(g_mix.ap()[l])
                    norm_transpose(g)
                    P.dma("sp", fbt[:], fbias.ap()[l].partition_broadcast(128), writes=[B_fbt])
                    tok0 = tb * TB

                    def fm_chunk(s, m, half, scale, dsts, bsl):
                        hb = [B_hT[4 * half + q] for q in range(4)]
                        b = nxt("acc", 6)
                        mm_group(acc[:, b], [(ring[:, s, j, m * 128:(m + 1) * 128], hT[:, j, half * 512:(half + 1) * 512]) for j in range(8)],
                                 hb + bsl, B_acc[b])
                        k = nxt("stg", 3)
                        P.op("act", lambda e, b=b, k=k: e.activation(out=stg[:, k], in_=acc[:, b], func=AF.Copy, scale=scale),
                             reads=[B_acc[b]], writes=[B_stg[k]])
                        for (dap, p0, p1, bd) in dsts:
                            P.dma("sp", dap, stg[p0:p1, k], reads=[B_stg[k]])

                    c0 = tok0
                    for which, off in ((0, OFF_FQ), (1, OFF_FK)):
                        s = ring_tile([(0, 512, wsrc([win[0], win[1]], l, 0, 8, off, off + 512), 8)])
                        for half in range(2):
                            for m in range(4):
                                cs = c0 + half * 512
                                if which == 0:
                                    dsts = [(QF.ap()[2 * m, 0:64, cs:cs + 512], 0, 64, None), (QF.ap()[2 * m + 1, 0:64, cs:cs + 512], 64, 128, None)]
                                else:
                                    dsts = [(KF[l][2 * m].ap()[0:64, cs:cs + 512], 0, 64, None), (KF[l][2 * m + 1].ap()[0:64, cs:cs + 512], 64, 128, None)]
                                fm_chunk(s, m, half, 0.125 if which == 0 else 1.0, dsts, [B_ring[s][0], B_ring[s][1]])
                    for which, off, offs in ((0, OFF_DQ, OFF_DQS), (1, OFF_DK, OFF_DKS)):
                        s1 = ring_tile([(0, 512, wsrc([win[0], win[1]], l, 0, 8, off, off + 512), 8)])
                        s2 = ring_tile([(0, 512, wsrc([win[0], win[1]], l, 0, 8, offs, offs + 512), 8)])
                        for half in range(2):
                            cs = c0 + half * 512
                            tg = half
                            if which == 0:
                                P.dma("sp", trig[:, tg, 0], cosT.ap()[:, cs:cs + 512], writes=[B_trig[tg]])
                                P.dma("sp", trig[:, tg, 1], sinT.ap()[:, cs:cs + 512], writes=[B_trig[tg]])
                            hb = [B_hT[4 * half + q] for q in range(4)]
                            for m in range(4):
                                b1 = nxt("acc", 6)
                                mm_group(acc[:, b1], [(ring[:, s1, j, m * 128:(m + 1) * 128], hT[:, j, half * 512:(half + 1) * 512]) for j in range(8)],
                                         hb + B_ring[s1], B_acc[b1])
                                b2 = nxt("acc", 6)
                                mm_group(acc[:, b2], [(ring[:, s2, j, m * 128:(m + 1) * 128], hT[:, j, half * 512:(half + 1) * 512]) for j in range(8)],
                                         hb + B_ring[s2], B_acc[b2])
                                sc = 0.125 if which == 0 else 1.0
                                k1 = nxt("rt", 2)
                                P.op("dve", lambda e, b1=b1, k1=k1, tg=tg, sc=sc: e.scalar_tensor_tensor(
                                    out=rt[:, k1], in0=acc[:, b1], scalar=sc, in1=trig[:, tg, 0], op0=ALU.mult, op1=ALU.mult),
                                     reads=[B_acc[b1], B_trig[tg]], writes=[B_rt[k1]])
                                k = nxt("stg", 3)
                                P.op("dve", lambda e, b2=b2, k=k, k1=k1, tg=tg, sc=sc: e.scalar_tensor_tensor(
                                    out=rt[:, 2], in0=acc[:, b2], scalar=sc, in1=trig[:, tg, 1], op0=ALU.mult, op1=ALU.mult),
                                     reads=[B_acc[b2], B_trig[tg]], writes=[B_rt[2]])
                                P.op("dve", lambda e, k=k, k1=k1: e.tensor_tensor(out=stg[:, k], in0=rt[:, k1], in1=rt[:, 2], op=ALU.add),
                                     reads=[B_rt[k1], B_rt[2]], writes=[B_stg[k]])
                                dap = QD.ap()[m, :, cs:cs + 512] if which == 0 else KD[l][m].ap()[:, cs:cs + 512]
                                P.dma("sp", dap, stg[:, k], reads=[B_stg[k]])
                    sv = ring_tile([(0, 512, wsrc([win[0], win[1]], l, 0, 8, OFF_FV, OFF_FV + 512), 8)])
                    sd = ring_tile([(0, 512, wsrc([win[0], win[1]], l, 0, 8, OFF_DV, OFF_DV + 512), 8)])
                    sf = ring_tile([(0, 8, wsrc([win[0], win[1]], l, 0, 8, OFF_FF, OFF_FF + 8), 8)])
                    for t in range(8):
                        row0 = tok0 + t * 128
                        k = t % 2
                        b = nxt("acc", 6)
                        mm_group(acc[:, b], [(hT[:, j, t * 128:(t + 1) * 128], ring[:, sv, j, :]) for j in range(8)], [B_hT[t]] + B_ring[sv], B_acc[b])
                        P.op("act", lambda e, b=b, k=k: e.activation(out=vft[:, k], in_=acc[:, b], func=AF.Copy),
                             reads=[B_acc[b]], writes=[B_vft[k]])
                        for hh in range(8):
                            P.dma("sp", VF[l][hh].ap()[row0:row0 + 128, :], vft[:, k, hh * 64:(hh + 1) * 64], reads=[B_vft[k]])
                        b = nxt("acc", 6)
                        mm_group(acc[:, b], [(hT[:, j, t * 128:(t + 1) * 128], ring[:, sd, j, :]) for j in range(8)], [B_hT[t]] + B_ring[sd], B_acc[b])
                        P.op("act", lambda e, b=b, k=k: e.activation(out=vdt[:, k], in_=acc[:, b], func=AF.Copy),
                             reads=[B_acc[b]], writes=[B_vdt[k]])
                        for hh in range(4):
                            P.dma("sp", VD[l][hh].ap()[row0:row0 + 128, :], vdt[:, k, hh * 128:(hh + 1) * 128], reads=[B_vdt[k]])
                        b = nxt("acc", 6)
                        mm_group(acc[:, b, 0:8], [(hT[:, j, t * 128:(t + 1) * 128], ring[:, sf, j, 0:8]) for j in range(8)], [B_hT[t], B_ring[sf][0]], B_acc[b])
                        P.op("dve", lambda e, b=b, t=t: e.tensor_tensor(out=zz[:, t], in0=acc[:, b, 0:8], in1=fbt[:], op=ALU.add),
                             reads=[B_acc[b], B_fbt], writes=[B_zz])
                    if 'lsig' in SKIP:
                        return
                    P.op("dve", lambda e: e.tensor_scalar(out=ze[:], in0=zz[:], scalar1=-1.0, scalar2=None, op0=ALU.mult), reads=[B_zz], writes=[B_ze])
                    P.op("dve", lambda e: e.tensor_tensor(out=za[:], in0=zz[:], in1=ze[:], op=ALU.max), reads=[B_zz, B_ze], writes=[B_za])
                    P.op("act", lambda e: e.activation(out=ze[:], in_=za[:], func=AF.Exp, scale=-1.0), reads=[B_za], writes=[B_ze])
                    P.op("act", lambda e: e.activation(out=za[:], in_=ze[:], func=AF.Ln, bias=oneb[:, 0:1], scale=1.0), reads=[B_ze], writes=[B_za])
                    P.op("dve", lambda e: e.scalar_tensor_tensor(out=lf[:], in0=zz[:], scalar=0.0, in1=za[:], op0=ALU.min, op1=ALU.subtract),
                         reads=[B_zz, B_za], writes=[B_lf])
                    if 'cumsum' in SKIP:
                        return
                    for t in range(8):
                        b = nxt("acc", 6)
                        P.op("pe", lambda e, b=b, t=t: e.matmul(acc[0:8, b, 0:128], lhsT=lf[:, t], rhs=triu[:], start=True, stop=True),
                             reads=[B_lf, B_triu], writes=[B_acc[b]])
                        P.op("dve", lambda e, b=b, t=t: e.tensor_copy(out=wT[:, tok0 + t * 128: tok0 + (t + 1) * 128], in_=acc[0:8, b, 0:128]),
                             reads=[B_acc[b]], writes=[B_wT])

                def final_norm(tb):
                    g = load_gain(g_fin.ap()[0])
                    for t in range(8):
                        P.op("act", lambda e, t=t: e.activation(out=junk[:], in_=xs[:, t], func=AF.Square, accum_out=ssq[:, t:t + 1]),
                             reads=[B_xs[t]], writes=[B_junk, B_ssq], noembed=True)
                    P.op("act", lambda e: e.activation(out=rms[:], in_=ssq[:], func=AF.Sqrt, bias=epsb[:, 0:1], scale=1.0 / D),
                         reads=[B_ssq], writes=[B_rms])
                    P.op("dve", lambda e: e.reciprocal(out=rstd[:], in_=rms[:]), reads=[B_rms], writes=[B_rstd])
                    for t in range(8):
                        k = t % 2
                        P.op("dve", lambda e, t=t, k=k: e.scalar_tensor_tensor(
                            out=yt[:, k], in0=xs[:, t], scalar=rstd[:, t:t + 1], in1=gain[:, g], op0=ALU.mult, op1=ALU.mult),
                            reads=[B_xs[t], B_rstd, B_gain[g]], writes=[B_yt[k]])
                        row0 = tb * TB + t * 128
                        P.dma("sp", y_out.ap()[row0:row0 + 128, :], yt[:, k], reads=[B_yt[k]])

                for tb in range(NTB):
                    src = x_in if first else X1
                    for t in range(8):
                        row0 = tb * TB + t * 128
                        P.dma("sp", xs[:, t], src.ap()[row0:row0 + 128, :], writes=[B_xs[t]])
                    if not first:
                        outproj(layer - 1, tb)
                        ffn(layer - 1, 1)
                    if last:
                        final_norm(tb)
                    else:
                        ffn(layer, 0)
                        for t in range(8):
                            row0 = tb * TB + t * 128
                            P.dma("sp", X1.ap()[row0:row0 + 128, :], xs[:, t], reads=[B_xs[t]])
                        if 'proj' not in SKIP:
                            proj(layer, tb)
                P.flush()

        def exchange(l):
            with ExitStack() as ps:
                sb = lambda name, shape, dtp: ps.enter_context(nc.sbuf_tensor(uniq(name), shape, dtp))
                tt = sb("tt", [8, NT], F32)
                tta = sb("tta", [8, 4, NT], F32)
                ttg = sb("ttg", [8, NT, 4], F32)
                onesf = sb("onesf", [8, 128], F32)
                pinc = sb("pinc", [8, NT, 4], F32)
                pex = sb("pex", [8, NT, 4], F32)
                oh = sb("oh", [8, 4], F32)
                pown = sb("pown", [8, NT], F32)
                cT = sb("cT", [8, NLOC], F32)
                r1 = sb("r1", [8, NLOC], F32)
                pq = sb("pq", [8, 3, NLOC], BF16)
                pk = sb("pk", [8, 3, NLOC], BF16)
                Bt = {k: Buf() for k in ("tt", "tta", "ttg", "onesf", "pinc", "pex", "oh", "pown", "cT", "r1", "pq", "pk", "TTI", "TTg")}
                P.op("dve", lambda e: e.tensor_copy(out=tt[:], in_=wT[:].rearrange("h (g t) -> h g t", t=128)[:, :, 127]),
                     reads=[B_wT], writes=[Bt["tt"]])
                t_tti = P.dma("sp", TTI[l].ap()[:, :], tt[:], reads=[Bt["tt"]], writes=[Bt["TTI"]])
                c_kd = [None] * 4
                c_vd = [None] * 4
                c_kd[0] = collective(KD[l][0], KDg[l][0], [])
                c_vd[0] = collective(VD[l][0], VDg[l][0], [])
                c_tt = collective(TTI[l], TTg[l], [t_tti])
                for h in range(1, 4):
                    c_kd[h] = collective(KD[l][h], KDg[l][h], [])
                    c_vd[h] = collective(VD[l][h], VDg[l][h], [])
                c_vf = [collective(VF[l][h], VFg[l][h], []) for h in range(8)]
                P.dma("sp", tta[:], TTg[l].ap().rearrange("r h g -> h r g"), writes=[Bt["tta"]], extra=[c_tt])
                P.dma("sp", oh[:], onehot.ap()[:, :], writes=[Bt["oh"]])
                P.op("dve", lambda e: e.memset(onesf[:], 1.0), writes=[Bt["onesf"]])
                P.op("dve", lambda e: e.tensor_copy(out=ttg[:], in_=tta[:].rearrange("h r g -> h g r")), reads=[Bt["tta"]], writes=[Bt["ttg"]])
                P.op("dve", lambda e: e.tensor_tensor_scan(out=pinc[:].rearrange("h g r -> h (g r)"), data0=onesf[:],
                                                           data1=ttg[:].rearrange("h g r -> h (g r)"), initial=0.0, op0=ALU.mult, op1=ALU.add),
                     reads=[Bt["ttg"], Bt["onesf"]], writes=[Bt["pinc"]])
                P.op("dve", lambda e: e.tensor_tensor(out=pex[:], in0=pinc[:], in1=ttg[:], op=ALU.subtract), reads=[Bt["pinc"], Bt["ttg"]], writes=[Bt["pex"]])
                P.op("dve", lambda e: e.tensor_tensor(out=pinc[:], in0=pex[:], in1=oh[:].unsqueeze(1).to_broadcast([8, NT, 4]), op=ALU.mult),
                     reads=[Bt["pex"], Bt["oh"]], writes=[Bt["pinc"]])
                P.op("dve", lambda e: e.tensor_reduce(out=pown[:], in_=pinc[:], axis=AX.X, op=ALU.add), reads=[Bt["pinc"]], writes=[Bt["pown"]])
                P.op("dve", lambda e: e.tensor_tensor(out=cT[:].rearrange("h (g t) -> h g t", t=128), in0=wT[:].rearrange("h (g t) -> h g t", t=128),
                                                      in1=pown[:].unsqueeze(2).to_broadcast([8, NT, 128]), op=ALU.add),
                     reads=[B_wT, Bt["pown"]], writes=[Bt["cT"]])
                P.op("dve", lambda e: e.tensor_copy(out=pq[:, 0], in_=cT[:]), reads=[Bt["cT"]], writes=[Bt["pq"]])
                P.op("dve", lambda e: e.tensor_tensor(out=r1[:], in0=cT[:], in1=pq[:, 0], op=ALU.subtract), reads=[Bt["cT"], Bt["pq"]], writes=[Bt["r1"]])
                P.op("dve", lambda e: e.tensor_copy(out=pq[:, 1], in_=r1[:]), reads=[Bt["r1"]], writes=[Bt["pq"]])
                P.op("dve", lambda e: e.tensor_tensor(out=cT[:], in0=r1[:], in1=pq[:, 1], op=ALU.subtract), reads=[Bt["r1"], Bt["pq"]], writes=[Bt["cT"]])
                P.op("dve", lambda e: e.tensor_copy(out=pq[:, 2], in_=cT[:]), reads=[Bt["cT"]], writes=[Bt["pq"]])
                P.op("dve", lambda e: e.tensor_scalar(out=pk[:], in0=pq[:], scalar1=-1.0, scalar2=None, op0=ALU.mult), reads=[Bt["pq"]], writes=[Bt["pk"]])
                P.dma("sp", QF.ap()[:, 64:67, :], pq[:], reads=[Bt["pq"]])
                c_kf = [None] * 8
                for h in range(8):
                    t_pk = P.dma("sp", KF[l][h].ap()[67:70, :].unsqueeze(0), pk[h:h + 1, :, :], reads=[Bt["pk"]])
                    c_kf[h] = collective(KF[l][h], KFg[l][h], [t_pk])
                P.flush()
            return {"kd": c_kd, "vd": c_vd, "vf": c_vf, "kf": c_kf}

        def attention(l, cc):
            lam_init = 0.8 - 0.6 * math.exp(-0.3 * l)
            with ExitStack() as ps:
                sb = lambda name, shape, dtp: ps.enter_context(nc.sbuf_tensor(uniq(name), shape, dtp))
                Ksb = sb("Ksb", [128, 2, 4, NLOC], BF16)
                Vsb = sb("Vsb", [128, 2, 4 * NT * 129], BF16)
                Qsb = sb("Qsb", [128, 2, NLOC], BF16)
                PT = sb("PT", [128, 3, 4, 512], BF16)
                Sm = sb("Sm", [128, 2, 4, 512], F32)
                mask = sb("mask", [128, 512], F32)
                lamt = sb("lamt", [128, 4, 64], F32)
                lprod = sb("lprod", [128, 2, 64], F32)
                lsum = sb("lsum", [128, 2], F32)
                lexp = sb("lexp", [128, 2], F32)
                neglam = sb("neglam", [128, 1], F32)
                gsub = sb("gsub", [128, 128], F32)
                nhalf = sb("nhalf", [128, 4], F32)
                rl = sb("rl", [128, 2, 4, 2], F32)
                ocf = sb("ocf", [128, 2, 4, 64], BF16)
                a1 = sb("a1", [128, 4, 128], F32)
                t2 = sb("t2", [128, 4, 128], F32)
                dd = sb("dd", [128, 4, 128], F32)
                ssqd = sb("ssqd", [128, 4], F32)
                rstdd = sb("rstdd", [128, 4], F32)
                ocd = sb("ocd", [128, 2, 4, 128], BF16)
                PS = ps.enter_context(nc.psum_tensor(uniq("PS"), [128, 8, 512], F32))
                B_bank = [Buf() for _ in range(8)]
                B_K = [[Buf() for _ in range(4)] for _ in range(2)]
                B_V = [[Buf() for _ in range(8)] for _ in range(2)]
                B_Q = [Buf(), Buf()]
                B_PT = [Buf() for _ in range(3)]
                B_Sm = [Buf(), Buf()]
                B_S = [Buf(), Buf()]
                B_O = Buf()
                Bm = {k: Buf() for k in ("mask", "lamt", "lprod", "lsum", "lexp", "neglam", "gsub", "nhalf", "rl0", "rl1", "ocf0", "ocf1",
                                         "a1", "t2", "dd", "ssqd", "rstdd", "ocd0", "ocd1")}
                P.dma("sp", mask[:], maskd.ap()[:, :], writes=[Bm["mask"]])
                P.dma("sp", lamt[:].rearrange("p a d -> p (a d)"), lam_in.ap()[l].rearrange("a d -> (a d)").partition_broadcast(128), writes=[Bm["lamt"]])
                P.dma("sp", gsub[:], subln.ap()[l].partition_broadcast(128), writes=[Bm["gsub"]])
                P.op("dve", lambda e: e.tensor_tensor(out=lprod[:], in0=lamt[:, 0:4:2, :], in1=lamt[:, 1:4:2, :], op=ALU.mult),
                     reads=[Bm["lamt"]], writes=[Bm["lprod"]])
                P.op("dve", lambda e: e.tensor_reduce(out=lsum[:], in_=lprod[:], axis=AX.X, op=ALU.add), reads=[Bm["lprod"]], writes=[Bm["lsum"]])
                P.op("act", lambda e: e.activation(out=lexp[:], in_=lsum[:], func=AF.Exp), reads=[Bm["lsum"]], writes=[Bm["lexp"]])
                P.op("dve", lambda e: e.tensor_tensor(out=neglam[:], in0=lexp[:, 1:2], in1=lexp[:, 0:1], op=ALU.subtract), reads=[Bm["lexp"]], writes=[Bm["neglam"]])
                P.op("dve", lambda e: e.tensor_scalar(out=neglam[:], in0=neglam[:], scalar1=-lam_init, scalar2=None, op0=ALU.add), reads=[], writes=[Bm["neglam"]])
                P.op("dve", lambda e: e.tensor_scalar(out=gsub[:], in0=gsub[:], scalar1=1.0 - lam_init, scalar2=None, op0=ALU.mult), reads=[], writes=[Bm["gsub"]])
                P.op("dve", lambda e: e.memset(nhalf[:], -0.5), writes=[Bm["nhalf"]])

                units = [("d", h) for h in range(4)] + [("f", h) for h in range(8)]

                vkind = [None, None]
                kkind = [None, None]

                def load_unit(ui):
                    kind, h = units[ui]
                    k = ui % 2
                    dv = 65 if kind == "f" else 129
                    if vkind[k] != kind:
                        vkind[k] = kind
                        P.op("dve", lambda e, k=k, dv=dv: e.memset(Vsb[:, k, 0:4 * NT * dv].rearrange("p (n d) -> p n d", d=dv)[:, :, dv - 1:dv], 1.0),
                             writes=B_V[k])
                    if kind == "f" and kkind[k] != "f":
                        kkind[k] = "f"
                        P.op("pool", lambda e, k=k: e.memset(Ksb[64:128, k], 0.0), writes=B_K[k])
                        P.op("pool", lambda e, k=k: e.memset(Qsb[64:128, k], 0.0), writes=[B_Q[k]])
                    if kind == "f":
                        for r in range(4):
                            P.dma("sp", Ksb[0:70, k, r, :], KFg[l][h].ap()[r], writes=[B_K[k][r]], extra=[cc["kf"][h]])
                        P.dma("sp", Qsb[0:70, k, :], QF.ap()[h, :, :], writes=[B_Q[k]])
                        for r in range(4):
                            src = VFg[l][h].ap()[r].rearrange("(g p) d -> p g d", p=128)
                            for hf in range(2):
                                P.dma("sp", Vsb[:, k, (r * NT + 16 * hf) * 65:(r * NT + 16 * hf + 16) * 65].rearrange("p (g d) -> p g d", d=65)[:, :, 0:64],
                                      src[:, 16 * hf:16 * hf + 16, :], writes=[B_V[k][2 * r + hf]], extra=[cc["vf"][h]])
                    else:
                        for r in range(4):
                            P.dma("sp", Ksb[:, k, r, :], KDg[l][h].ap()[r], writes=[B_K[k][r]], extra=[cc["kd"][h]])
                        P.dma("sp", Qsb[:, k, :], QD.ap()[h, :, :], writes=[B_Q[k]])
                        for r in range(4):
                            src = VDg[l][h].ap()[r].rearrange("(g p) d -> p g d", p=128)
                            for hf in range(2):
                                P.dma("sp", Vsb[:, k, (r * NT + 16 * hf) * 129:(r * NT + 16 * hf + 16) * 129].rearrange("p (g d) -> p g d", d=129)[:, :, 0:128],
                                      src[:, 16 * hf:16 * hf + 16, :], writes=[B_V[k][2 * r + hf]], extra=[cc["vd"][h]])

                cnt = {"S": 0, "PT": 0, "Sm": 0, "oc": 0}

                def run_unit(ui):
                    kind, h = units[ui]
                    kb = ui % 2
                    dv = 65 if kind == "f" else 129
                    KR = 128 if kind == "f" else 64
                    if kind == "f":
                        sbufs = [(0, 4), (4, 3)]
                        obanks = [7]
                    else:
                        sbufs = [(0, 2), (2, 2)]
                        obanks = [4, 5, 6, 7]
                    B_Ob = [B_bank[o] for o in obanks]

                    def Kap(c, r, g):
                        p0 = 0 if kind == "f" else 64 * c
                        return Ksb[p0:p0 + KR, kb, r, g * 128:(g + 1) * 128]

                    def Qap(c, c0, c1):
                        p0 = 0 if kind == "f" else 64 * c
                        return Qsb[p0:p0 + KR, kb, c0:c1]

                    def Vap(r, g):
                        return Vsb[:, kb, (r * NT + g) * dv:(r * NT + g + 1) * dv]

                    def Oap(a, c):
                        if kind == "f":
                            return PS[:, 7, 128 * a:128 * a + 65]
                        return PS[:, 4 + a, 256 * c:256 * c + 129]

                    ncomp = 1 if kind == "f" else 2
                    for J in range(8):
                        segs = []
                        q0 = 512 * J
                        for g in range(4 * J + 3):
                            b = g - 4 * J
                            a_lo = 0 if b < 0 else b + 1
                            N = 128 * (4 - a_lo)
                            qa, qb = q0 + 128 * a_lo, q0 + 512
                            for r in range(4):
                                for c in range(ncomp):
                                    segs.append((N, False, [(0, N, Kap(c, r, g), Qap(c, qa, qb))],
                                                 [((a - a_lo) * 128, Vap(r, g), Oap(a, c)) for a in range(a_lo, 4)]))
                        for b in range(4):
                            g = 4 * J + b
                            for c in range(ncomp):
                                segs.append((512, True,
                                             [(r * 128, 128, Kap(c, r, g), Qap(c, q0 + 128 * b, q0 + 128 * b + 128)) for r in range(4)],
                                             [(r * 128, Vap(r, g), Oap(b, c)) for r in range(4)]))
                        tiles = []
                        sidx = cnt["S"]
                        i = 0
                        while i < len(segs):
                            cap = sbufs[sidx % 2][1]
                            j = i + 1
                            while j < len(segs) and j - i < cap and segs[j][0] == segs[i][0] and segs[j][1] == segs[i][1]:
                                j += 1
                            tiles.append(segs[i:j])
                            sidx += 1
                            i = j

                        nt_ = len(tiles)
                        info = [None] * nt_

                        def emit_qk(n):
                            tsegs = tiles[n]
                            ns = len(tsegs)
                            N, masked = tsegs[0][0], tsegs[0][1]
                            bank0 = sbufs[cnt["S"] % 2][0]
                            cnt["S"] += 1
                            pti = cnt["PT"] % 3
                            cnt["PT"] += 1
                            bsb = [B_bank[bank0 + si] for si in range(ns)]

                            def fqk(e, tsegs=tsegs, bank0=bank0):
                                ins = None
                                for si, sg_ in enumerate(tsegs):
                                    for (c0, w, lap, rap) in sg_[2]:
                                        ins = e.matmul(PS[:, bank0 + si, c0:c0 + w], lhsT=lap, rhs=rap, start=True, stop=True)
                                return ins
                            P.op("pe", fqk, reads=B_K[kb] + [B_Q[kb]], writes=bsb)
                            if masked:
                                smi = cnt["Sm"] % 2
                                cnt["Sm"] += 1
                                P.op("dve", lambda e, bank0=bank0, ns=ns, smi=smi: e.tensor_tensor(
                                    out=Sm[:, smi, 0:ns], in0=PS[:, bank0:bank0 + ns, :], in1=mask[:].unsqueeze(1).to_broadcast([128, ns, 512]), op=ALU.add),
                                    reads=bsb + [Bm["mask"]], writes=[B_Sm[smi]])
                                P.op("act", lambda e, smi=smi, pti=pti, ns=ns: e.activation(out=PT[:, pti, 0:ns], in_=Sm[:, smi, 0:ns], func=AF.Exp),
                                     reads=[B_Sm[smi]], writes=[B_PT[pti]])
                            else:
                                P.op("act", lambda e, bank0=bank0, ns=ns, pti=pti, N=N: e.activation(
                                    out=PT[:, pti, 0:ns, 0:N], in_=PS[:, bank0:bank0 + ns, 0:N], func=AF.Exp),
                                     reads=bsb, writes=[B_PT[pti]])
                            info[n] = pti

                        def emit_pv(n, first, lastt):
                            tsegs = tiles[n]
                            pti = info[n]

                            def fpv(e, tsegs=tsegs, pti=pti, first=first, lastt=lastt):
                                ins = None
                                if first:
                                    if kind == "f":
                                        e.matmul(PS[:, 7, 0:449], lhsT=zer_bf[:, 0:128], rhs=zer_bf[:, 0:449], start=True, stop=False, skip_group_check=True)
                                    else:
                                        for a in range(4):
                                            e.matmul(PS[:, 4 + a, 0:385], lhsT=zer_bf[:, 0:128], rhs=zer_bf[:, 0:385], start=True, stop=False, skip_group_check=True)
                                for si, sg_ in enumerate(tsegs):
                                    for (c0, vap, oap) in sg_[3]:
                                        ins = e.matmul(oap, lhsT=PT[:, pti, si, c0:c0 + 128], rhs=vap, start=False, stop=lastt, skip_group_check=True)
                                return ins
                            P.op("pe", fpv, reads=[B_PT[pti], B_zer] + B_V[kb], writes=B_Ob)

                        for n in range(nt_ + 1):
                            if n < nt_:
                                emit_qk(n)
                            if n >= 1:
                                emit_pv(n - 1, n - 1 == 0, n - 1 == nt_ - 1)

                        oi = cnt["oc"] % 2
                        cnt["oc"] += 1
                        rows = OC.ap()[q0:q0 + 512, :].rearrange("(a p) d -> p a d", p=128)
                        if kind == "f":
                            O7 = PS[:, 7, :].rearrange("p (a w) -> p a w", w=128)
                            P.op("dve", lambda e, oi=oi, O7=O7: e.reciprocal(out=rl[:, oi, :, 0:1], in_=O7[:, :, 64:65]), reads=B_Ob, writes=[Bm[f"rl{oi}"]])
                            P.op("dve", lambda e, oi=oi, O7=O7: e.tensor_tensor(out=ocf[:, oi], in0=O7[:, :, 0:64], in1=rl[:, oi, :, 0:1].to_broadcast([128, 4, 64]), op=ALU.mult),
                                 reads=B_Ob + [Bm[f"rl{oi}"]], writes=[Bm[f"ocf{oi}"]])
                            P.dma("sp", rows[:, :, h * 64:(h + 1) * 64], ocf[:, oi], reads=[Bm[f"ocf{oi}"]])
                        else:
                            O4 = PS[:, 4:8, :].rearrange("p a (c w) -> p a c w", c=2)
                            P.op("dve", lambda e, oi=oi, O4=O4: e.reciprocal(out=rl[:, oi].unsqueeze(3), in_=O4[:, :, :, 128:129]), reads=B_Ob, writes=[Bm[f"rl{oi}"]])
                            P.op("dve", lambda e, oi=oi, O4=O4: e.tensor_tensor(out=a1[:], in0=O4[:, :, 0, 0:128], in1=rl[:, oi, :, 0:1].to_broadcast([128, 4, 128]), op=ALU.mult),
                                 reads=B_Ob + [Bm[f"rl{oi}"]], writes=[Bm["a1"]])
                            P.op("dve", lambda e, oi=oi: e.tensor_scalar(out=rl[:, oi, :, 1:2], in0=rl[:, oi, :, 1:2], scalar1=neglam[:, 0:1], scalar2=None, op0=ALU.mult),
                                 reads=[Bm["neglam"]], writes=[Bm[f"rl{oi}"]])
                            P.op("dve", lambda e, oi=oi, O4=O4: e.tensor_tensor(out=t2[:], in0=O4[:, :, 1, 0:128], in1=rl[:, oi, :, 1:2].to_broadcast([128, 4, 128]), op=ALU.mult),
                                 reads=B_Ob + [Bm[f"rl{oi}"]], writes=[Bm["t2"]])
                            P.op("dve", lambda e: e.tensor_tensor(out=dd[:], in0=a1[:], in1=t2[:], op=ALU.add), reads=[Bm["a1"], Bm["t2"]], writes=[Bm["dd"]])
                            P.op("dve", lambda e: e.tensor_tensor(out=a1[:], in0=dd[:], in1=dd[:], op=ALU.mult), reads=[Bm["dd"]], writes=[Bm["a1"]])
                            P.op("dve", lambda e: e.tensor_reduce(out=ssqd[:], in_=a1[:], axis=AX.X, op=ALU.add), reads=[Bm["a1"]], writes=[Bm["ssqd"]])
                            P.op("dve", lambda e: e.tensor_scalar(out=ssqd[:], in0=ssqd[:], scalar1=1.0 / 128, scalar2=EPS, op0=ALU.mult, op1=ALU.add),
                                 reads=[], writes=[Bm["ssqd"]])
                            P.op("pool", lambda e: e.tensor_tensor(out=rstdd[:], in0=ssqd[:], in1=nhalf[:], op=ALU.pow), reads=[Bm["ssqd"], Bm["nhalf"]], writes=[Bm["rstdd"]])
                            P.op("dve", lambda e: e.tensor_tensor(out=t2[:], in0=dd[:], in1=rstdd[:].unsqueeze(2).to_broadcast([128, 4, 128]), op=ALU.mult),
                                 reads=[Bm["dd"], Bm["rstdd"]], writes=[Bm["t2"]])
                            P.op("dve", lambda e, oi=oi: e.tensor_tensor(out=ocd[:, oi], in0=t2[:], in1=gsub[:].unsqueeze(1).to_broadcast([128, 4, 128]), op=ALU.mult),
                                 reads=[Bm["t2"], Bm["gsub"]], writes=[Bm[f"ocd{oi}"]])
                            P.dma("sp", rows[:, :, 512 + h * 128:512 + (h + 1) * 128], ocd[:, oi], reads=[Bm[f"ocd{oi}"]])

                load_unit(0)
                for ui in range(len(units)):
                    if ui + 1 < len(units):
                        load_unit(ui + 1)
                    run_unit(ui)
                P.flush()

        def run_all():
            n = 0
            phase_tokens(0, True, False); n += 1
            if n >= stop: return
            cc = exchange(0); n += 1
            if n >= stop: return
            attention(0, cc); n += 1
            if n >= stop: return
            phase_tokens(1, False, False); n += 1
            if n >= stop: return
            cc = exchange(1); n += 1
            if n >= stop: return
            attention(1, cc); n += 1
            if n >= stop: return
            phase_tokens(2, False, True)
        run_all()
        if debug:
            P.dma("sp", dbg["X1"].ap(), X1.ap())
            if 'proj' not in SKIP:
                if stop < 2:
                    P.dma("sp", dbg["QF"].ap()[:, 0:64, :], QF.ap()[:, 0:64, :])
                else:
                    P.dma("sp", dbg["QF"].ap(), QF.ap())
                P.dma("sp", dbg["QD"].ap(), QD.ap())
                for h in range(8):
                    if stop < 2:
                        P.dma("sp", dbg["KF"].ap()[h, 0:64, :], KF[0][h].ap()[0:64, :])
                    else:
                        P.dma("sp", dbg["KF"].ap()[h], KF[0][h].ap())
                    P.dma("sp", dbg["VF"].ap()[h], VF[0][h].ap())
                for h in range(4):
                    P.dma("sp", dbg["KD"].ap()[h], KD[0][h].ap())
                    P.dma("sp", dbg["VD"].ap()[h], VD[0][h].ap())
                if stop >= 2:
                    for r in range(4):
                        P.dma("sp", dbg["KFg"].ap()[r], KFg[0][3].ap()[r], extra=[(ccsem, ccsem.cnt)])
                        P.dma("sp", dbg["VDg"].ap()[r], VDg[0][2].ap()[r], extra=[(ccsem, ccsem.cnt)])
                if stop >= 3:
                    P.dma("sp", dbg["OC"].ap(), OC.ap())
            P.flush()
    return nc


def _rope_tables(r):
    inv_freq = (1.0 / (10000.0 ** (np.arange(0, 64, 2, dtype=np.float32) / np.float32(64)))).astype(np.float32)
    i = np.arange(NT)[:, None]
    p = np.arange(128)[None, :]
    pos = ((4 * i + r) * 128 + p).reshape(-1).astype(np.float32)
    freqs = (pos[:, None] * inv_freq[None, :]).astype(np.float32)
    cos = np.cos(freqs).astype(np.float32)
    sin = np.sin(freqs).astype(np.float32)
    d = np.arange(128) % 64
    fi = d % 32
    sign = np.where(d < 32, -1.0, 1.0).astype(np.float32)
    cosT = np.ascontiguousarray(cos[:, fi].T)
    sinT = np.ascontiguousarray((sin[:, fi] * sign[None, :]).T)
    return cosT, sinT


def _swap_cols(w):
    s = w.reshape(w.shape[:-1] + (8, 2, 32))[..., ::-1, :]
    return s.reshape(w.shape)


_NC_CACHE = {}


def kernel(x, ffn1_norm, ffn1_w_gate_up, ffn1_w_down, mix_norm, w_in, forget_bias,
           lambda_q1, lambda_k1, lambda_q2, lambda_k2, diff_subln, w_out,
           ffn2_norm, ffn2_w_gate_up, ffn2_w_down, final_norm, _debug=False, _stop=99):
    f32 = lambda a: np.ascontiguousarray(np.asarray(a, dtype=np.float32))
    x = f32(x)
    w_in = f32(w_in)
    win_ext = np.ascontiguousarray(np.concatenate(
        [w_in, _swap_cols(w_in[..., OFF_DQ:OFF_DQ + 512]), _swap_cols(w_in[..., OFF_DK:OFF_DK + 512])], axis=-1))
    lam_in = np.ascontiguousarray(np.stack([f32(lambda_q1), f32(lambda_k1), f32(lambda_q2), f32(lambda_k2)], axis=1))
    common = {
        "wgu1": f32(ffn1_w_gate_up), "wgu2": f32(ffn2_w_gate_up), "wd1": f32(ffn1_w_down), "wd2": f32(ffn2_w_down),
        "win": win_ext, "wout": f32(w_out), "g_ffn1": f32(ffn1_norm), "g_mix": f32(mix_norm), "g_ffn2": f32(ffn2_norm),
        "g_fin": f32(final_norm).reshape(1, D), "fbias": f32(forget_bias), "lam_in": lam_in, "subln": f32(diff_subln),
        "identd": np.eye(128, dtype=np.float32).astype(ml_dtypes.bfloat16),
        "triud": np.triu(np.ones((128, 128), dtype=np.float32)),
    }
    in_maps = []
    for c in range(8):
        b, r = c // 4, c % 4
        xl = np.ascontiguousarray(x[b].reshape(NT, 4, 128, D)[:, r].reshape(NLOC, D))
        cosT, sinT = _rope_tables(r)
        kk = np.arange(128)[:, None]
        qq = np.arange(128)[None, :]
        m = np.zeros((128, 4, 128), dtype=np.float32)
        for rp in range(4):
            if rp > r:
                m[:, rp, :] = NEG
            elif rp == r:
                m[:, rp, :] = np.where(kk <= qq, 0.0, NEG)
        oh = np.zeros((8, 4), dtype=np.float32)
        oh[:, r] = 1.0
        d = dict(common)
        d.update({"x": xl, "cosT": cosT, "sinT": sinT, "maskd": m.reshape(128, 512), "onehot": oh})
        in_maps.append(d)
    if _stop == -1:
        return in_maps
    key = (bool(_debug), _stop, tuple(sorted(SKIP)))
    if key not in _NC_CACHE:
        _NC_CACHE[key] = build(debug=_debug, stop=_stop)
    nc = _NC_CACHE[key]
    res = run_bass_kernel_spmd(nc, in_maps, core_ids=list(range(8)))
    out = np.empty((2, NT * 4 * 128, D), dtype=np.float32)
    ov = out.reshape(2, NT, 4, 128, D)
    for c in range(8):
        b, r = c // 4, c % 4
        ov[b, :, r] = np.asarray(res.results[c]["y"], dtype=np.float32).reshape(NT, 128, D)
    if _debug:
        return out, res
    return out
```

```python
import math
import numpy as np
import ml_dtypes
import concourse.bass as bass
import concourse.mybir as mybir
from concourse.bass_utils import run_bass_kernel_spmd

F32 = mybir.dt.float32
BF16 = mybir.dt.bfloat16
AF = mybir.ActivationFunctionType
ALU = mybir.AluOpType
AX = mybir.AxisListType

D = 1024
DFF = 2816
NCH = 22
NLOC = 4096
NT = 32
TB = 1024
NTB = NLOC // TB
EPS = 1e-5
INC = 4104
OFF_FQ, OFF_FK, OFF_FV, OFF_FF, OFF_DQ, OFF_DK, OFF_DV, OFF_DQS, OFF_DKS = 0, 512, 1024, 1536, 1544, 2056, 2568, 3080, 3592
NEG = -30000.0
SKIP = set()


class Sem:
    def __init__(self, h):
        self.h = h
        self.cnt = 0


class Buf:
    __slots__ = ("w", "r")

    def __init__(self):
        self.w = None
        self.r = []


class Prog:
    ENG = ("pe", "act", "dve", "pool", "sp")

    def __init__(self, nc, sems):
        self.nc = nc
        self.es = {e: Sem(sems[i]) for i, e in enumerate(self.ENG)}
        n = len(self.ENG)
        self.dpool = {"pool": [Sem(h) for h in sems[n:n + 12]], "sp": [Sem(h) for h in sems[n + 12:n + 36]],
                      "act": [Sem(h) for h in sems[n + 36:n + 40]]}
        self.di = {"pool": 0, "sp": 0, "act": 0}
        self.ops = {e: [] for e in self.ENG}
        self.waited = {e: {} for e in self.ENG}
        self.last_tok = {}

    def _deps(self, reads, writes, extra):
        w = []
        for b in reads:
            if b.w is not None:
                w.append(b.w)
        for b in writes:
            w.extend(b.r)
            if b.w is not None:
                w.append(b.w)
        w.extend(t for t in extra if t is not None)
        return w

    def _upd(self, tok, reads, writes):
        for b in reads:
            b.r.append(tok)
        for b in writes:
            b.w = tok
            b.r = []

    def op(self, eng, fn, reads=(), writes=(), extra=(), noembed=False):
        if noembed:
            fn.noembed = True
        s = self.es[eng]
        waits = self._deps(reads, writes, extra)
        s.cnt += 1
        tok = (s, s.cnt)
        self.ops[eng].append((fn, waits, s, 1))
        self._upd(tok, reads, writes)
        return tok

    def dma(self, eng, out, in_, reads=(), writes=(), extra=(), **kw):
        pool = self.dpool[eng]
        ds = pool[self.di[eng] % len(pool)]
        self.di[eng] += 1
        waits = self._deps(reads, writes, extra)
        if ds.cnt > 0:
            waits.append((ds, 16 * ds.cnt))
        ds.cnt += 1
        tok = (ds, 16 * ds.cnt)

        def fn(e, out=out, in_=in_, kw=kw):
            return e.dma_start(out=out, in_=in_, **kw)
        self.ops[eng].append((fn, waits, ds, 16))
        self._upd(tok, reads, writes)
        return tok

    def raw(self, eng, fn, extra=()):
        self.ops[eng].append((fn, [t for t in extra if t is not None], None, 0))

    def flush(self):
        nc = self.nc
        engs = {"pe": "tensor", "act": "scalar", "dve": "vector", "pool": "gpsimd", "sp": "sync"}
        all_sems = list(self.es.values()) + [d for p in self.dpool.values() for d in p]
        finals = [(s, s.cnt if s in self.es.values() else 16 * s.cnt) for s in all_sems]
        with nc.Block() as block:
            for ename in self.ENG:
                def body(e, ename=ename):
                    wd = self.waited[ename]
                    mine = self.es[ename]
                    for fn, waits, sem, inc in self.ops[ename]:
                        need = {}
                        for (s, v) in waits:
                            if s is mine and ename == "pe":
                                continue
                            if wd.get(s, 0) >= v:
                                continue
                            if need.get(s, 0) < v:
                                need[s] = v
                        items = list(need.items())
                        embed = None
                        if items and inc == 1 and ename in ("act", "dve", "pool") and not getattr(fn, "noembed", False):
                            embed = items.pop()
                        for s, v in items:
                            e.wait_ge(s.h, v)
                            wd[s] = v
                        ins = fn(e)
                        if embed is not None:
                            ins._wait_ge(embed[0].h, embed[1])
                            wd[embed[0]] = embed[1]
                        if sem is not None:
                            ins.then_inc(sem.h, inc)
                    for s, v in finals:
                        if v > 0 and wd.get(s, 0) < v:
                            e.wait_ge(s.h, v)
                            wd[s] = v
                getattr(block, engs[ename])(body)
        self.ops = {e: [] for e in self.ENG}


def build(debug=False, stop=99):
    nc = bass.Bass("TRN2", target_bir_lowering=False)
    dt = nc.dram_tensor
    x_in = dt("x", [NLOC, D], F32, kind="ExternalInput")
    wgu = [dt("wgu1", [2, D, 2 * DFF], F32, kind="ExternalInput"), dt("wgu2", [2, D, 2 * DFF], F32, kind="ExternalInput")]
    wdn = [dt("wd1", [2, DFF, D], F32, kind="ExternalInput"), dt("wd2", [2, DFF, D], F32, kind="ExternalInput")]
    win = dt("win", [2, D, INC], F32, kind="ExternalInput")
    wout = dt("wout", [2, D, D], F32, kind="ExternalInput")
    g_ffn1 = dt("g_ffn1", [2, D], F32, kind="ExternalInput")
    g_mix = dt("g_mix", [2, D], F32, kind="ExternalInput")
    g_ffn2 = dt("g_ffn2", [2, D], F32, kind="ExternalInput")
    g_fin = dt("g_fin", [1, D], F32, kind="ExternalInput")
    fbias = dt("fbias", [2, 8], F32, kind="ExternalInput")
    lam_in = dt("lam_in", [2, 4, 64], F32, kind="ExternalInput")
    subln = dt("subln", [2, 128], F32, kind="ExternalInput")
    cosT = dt("cosT", [128, NLOC], F32, kind="ExternalInput")
    sinT = dt("sinT", [128, NLOC], F32, kind="ExternalInput")
    maskd = dt("maskd", [128, 512], F32, kind="ExternalInput")
    onehot = dt("onehot", [8, 4], F32, kind="ExternalInput")
    identd = dt("identd", [128, 128], BF16, kind="ExternalInput")
    triud = dt("triud", [128, 128], F32, kind="ExternalInput")
    y_out = dt("y", [NLOC, D], F32, kind="ExternalOutput")

    X1 = dt("X1", [NLOC, D], F32)
    OC = dt("OC", [NLOC, D], BF16)
    QF = dt("QF", [8, 70, NLOC], BF16)
    QD = dt("QD", [4, 128, NLOC], BF16)
    KF = [[dt(f"KF{l}_{h}", [70, NLOC], BF16) for h in range(8)] for l in range(2)]
    VF = [[dt(f"VF{l}_{h}", [NLOC, 64], BF16) for h in range(8)] for l in range(2)]
    KD = [[dt(f"KD{l}_{h}", [128, NLOC], BF16) for h in range(4)] for l in range(2)]
    VD = [[dt(f"VD{l}_{h}", [NLOC, 128], BF16) for h in range(4)] for l in range(2)]
    TTI = [dt(f"TTI{l}", [8, NT], F32) for l in range(2)]
    KFg = [[dt(f"KFg{l}_{h}", [4, 70, NLOC], BF16) for h in range(8)] for l in range(2)]
    VFg = [[dt(f"VFg{l}_{h}", [4, NLOC, 64], BF16) for h in range(8)] for l in range(2)]
    KDg = [[dt(f"KDg{l}_{h}", [4, 128, NLOC], BF16) for h in range(4)] for l in range(2)]
    VDg = [[dt(f"VDg{l}_{h}", [4, NLOC, 128], BF16) for h in range(4)] for l in range(2)]
    TTg = [dt(f"TTg{l}", [4, 8, NT], F32) for l in range(2)]
    dbg = {}
    if debug:
        dbg["QF"] = dt("dbgQF", [8, 70, NLOC], BF16, kind="ExternalOutput")
        dbg["QD"] = dt("dbgQD", [4, 128, NLOC], BF16, kind="ExternalOutput")
        dbg["X1"] = dt("dbgX1", [NLOC, D], F32, kind="ExternalOutput")
        dbg["OC"] = dt("dbgOC", [NLOC, D], BF16, kind="ExternalOutput")
        dbg["KF"] = dt("dbgKF", [8, 70, NLOC], BF16, kind="ExternalOutput")
        dbg["VF"] = dt("dbgVF", [8, NLOC, 64], BF16, kind="ExternalOutput")
        dbg["KD"] = dt("dbgKD", [4, 128, NLOC], BF16, kind="ExternalOutput")
        dbg["VD"] = dt("dbgVD", [4, NLOC, 128], BF16, kind="ExternalOutput")
        dbg["KFg"] = dt("dbgKFg", [4, 70, NLOC], BF16, kind="ExternalOutput")
        dbg["VDg"] = dt("dbgVDg", [4, NLOC, 128], BF16, kind="ExternalOutput")

    groups = [[0, 1, 2, 3], [4, 5, 6, 7]]
    _u = [0]

    def uniq(name):
        _u[0] += 1
        return f"{name}_{_u[0]}"
    NSEM = 5 + 40 + 12
    from contextlib import ExitStack
    with ExitStack() as gs:
        sems = [gs.enter_context(nc.semaphore(f"s{i}")) for i in range(NSEM)]
        P = Prog(nc, sems)
        ccsem = Sem(sems[45])

        def collective(inp, outp, extra):
            ccsem.cnt += 1

            def fn(e, inp=inp, outp=outp):
                e.collective_compute("AllGather", ALU.bypass, replica_groups=groups,
                                     ins=[inp.ap().opt()], outs=[outp.ap().opt()]).then_inc(ccsem.h, 1)
                return None
            P.raw("pool", fn, extra=extra)
            return (ccsem, ccsem.cnt)

        ident = gs.enter_context(nc.sbuf_tensor("ident", [128, 128], BF16))
        zer_bf = gs.enter_context(nc.sbuf_tensor("zer_bf", [128, 512], BF16))
        wT = gs.enter_context(nc.sbuf_tensor("wT", [8, NLOC], F32))
        epsb = gs.enter_context(nc.sbuf_tensor("epsb", [128, 1], F32))
        oneb = gs.enter_context(nc.sbuf_tensor("oneb", [128, 1], F32))
        B_ident, B_ones, B_zer, B_wT = Buf(), Buf(), Buf(), Buf()
        with nc.sbuf_tensor("ones_bf", [8, 3, NLOC], BF16) as ones_bf:
            P.dma("sp", ident[:], identd.ap()[:, :], writes=[B_ident])
            P.op("dve", lambda e: e.memset(ones_bf[:], 1.0), writes=[B_ones])
            P.op("dve", lambda e: e.memset(zer_bf[:], 0.0), writes=[B_zer])
            P.op("dve", lambda e: e.memset(epsb[:], EPS))
            P.op("dve", lambda e: e.memset(oneb[:], 1.0))
            P.dma("sp", QF.ap()[:, 67:70, :], ones_bf[:], reads=[B_ones])
            for l in range(2):
                for h in range(8):
                    P.dma("sp", KF[l][h].ap()[64:67, :], ones_bf[0:3, 0, :], reads=[B_ones])
            P.flush()

        def phase_tokens(layer, first, last):
            with ExitStack() as ps:
                sb = lambda name, shape, dtp: ps.enter_context(nc.sbuf_tensor(uniq(name), shape, dtp))
                xs = sb("xs", [128, 8, D], F32)
                hT = sb("hT", [128, 8, TB], BF16)
                AT = sb("AT", [128, NCH, TB], BF16)
                RING = 6
                ring = sb("ring", [128, RING, 8, 512], BF16)
                gain = sb("gain", [128, 2, D], F32)
                hn = sb("hn", [128, 2, D], BF16)
                junk = sb("junk", [128, D], BF16)
                ssq = sb("ssq", [128, 8], F32)
                rms = sb("rms", [128, 8], F32)
                rstd = sb("rstd", [128, 8], F32)
                sg = sb("sg", [128, 2, 512], F32)
                ocl = sb("ocl", [128, 2, D], BF16) if not first else None
                trig = sb("trig", [128, 2, 2, 512], F32) if not last else None
                rt = sb("rt", [128, 3, 512], F32) if not last else None
                stg = sb("stg", [128, 3, 512], BF16) if not last else None
                vft = sb("vft", [128, 2, 512], BF16) if not last else None
                vdt = sb("vdt", [128, 2, 512], BF16) if not last else None
                fbt = sb("fbt", [128, 8], F32)
                zz = sb("zz", [128, 8, 8], F32)
                za = sb("za", [128, 8, 8], F32)
                ze = sb("ze", [128, 8, 8], F32)
                lf = sb("lf", [128, 8, 8], F32)
                triu = sb("triu", [128, 128], F32)
                yt = sb("yt", [128, 2, D], F32) if last else None
                tp = ps.enter_context(nc.psum_tensor(uniq("tp"), [128, 2, 8, 128], BF16))
                acc = ps.enter_context(nc.psum_tensor(uniq("acc"), [128, 6, 512], F32))

                B_xs = [Buf() for _ in range(8)]
                B_hT = [Buf() for _ in range(8)]
                B_AT = [[Buf(), Buf()] for _ in range(NCH)]
                B_ring = [[Buf(), Buf()] for _ in range(RING)]
                B_gain = [Buf(), Buf()]
                B_hn = [Buf(), Buf()]
                B_junk, B_ssq, B_rms, B_rstd = Buf(), Buf(), Buf(), Buf()
                B_sg = [Buf(), Buf()]
                B_ocl = [Buf(), Buf()]
                B_trig = [Buf(), Buf()]
                B_rt = [Buf(), Buf(), Buf()]
                B_stg = [Buf(), Buf(), Buf()]
                B_vft = [Buf(), Buf()]
                B_vdt = [Buf(), Buf()]
                B_fbt, B_zz, B_za, B_ze, B_lf, B_triu = Buf(), Buf(), Buf(), Buf(), Buf(), Buf()
                B_yt = [Buf(), Buf()]
                B_tp = [Buf(), Buf()]
                B_acc = [Buf() for _ in range(6)]
                st = {"ring": 0, "acc": 0, "gain": 0, "hn": 0, "sg": 0, "stg": 0, "rt": 0}

                def nxt(k, n):
                    v = st[k] % n
                    st[k] += 1
                    return v

                def ring_tile(parts):
                    s = nxt("ring", RING)
                    for (c0, c1, src, nj) in parts:
                        bufs = []
                        if c0 < 256:
                            bufs.append(B_ring[s][0])
                        if c1 > 256:
                            bufs.append(B_ring[s][1])
                        P.dma("pool", ring[:, s, 0:nj, c0:c1], src, writes=bufs)
                    return s

                def wsrc(wh, l, r0, nj, c0, c1):
                    return wh[l].rearrange("(j p) n -> p j n", p=128)[:, r0:r0 + nj, c0:c1]

                P.dma("sp", triu[:], triud[:, :], writes=[B_triu])

                def load_gain(src_ap):
                    g = nxt("gain", 2)
                    P.dma("sp", gain[:, g], src_ap.partition_broadcast(128), writes=[B_gain[g]])
                    return g

                def norm_transpose(g):
                    for t in range(8):
                        P.op("act", lambda e, t=t: e.activation(out=junk[:], in_=xs[:, t], func=AF.Square,
                                                                accum_out=ssq[:, t:t + 1]),
                             reads=[B_xs[t]], writes=[B_junk, B_ssq], noembed=True)
                    P.op("act", lambda e: e.activation(out=rms[:], in_=ssq[:], func=AF.Sqrt, bias=epsb[:, 0:1], scale=1.0 / D),
                         reads=[B_ssq], writes=[B_rms])
                    P.op("dve", lambda e: e.reciprocal(out=rstd[:], in_=rms[:]), reads=[B_rms], writes=[B_rstd])
                    for t in range(8):
                        k = nxt("hn", 2)
                        P.op("dve", lambda e, t=t, k=k: e.scalar_tensor_tensor(
                            out=hn[:, k], in0=xs[:, t], scalar=rstd[:, t:t + 1], in1=gain[:, g],
                            op0=ALU.mult, op1=ALU.mult),
                            reads=[B_xs[t], B_rstd, B_gain[g]], writes=[B_hn[k]])
                        transpose_to_hT(hn, k, B_hn[k], t)

                def transpose_to_hT(src, k, bsrc, t):
                    def tfn(e, k=k, src=src):
                        ins = None
                        for j in range(8):
                            ins = e.transpose(out=tp[:, k, j, :], in_=src[:, k, j * 128:(j + 1) * 128], identity=ident[:])
                        return ins
                    P.op("pe", tfn, reads=[bsrc, B_ident], writes=[B_tp[k]])
                    P.op("act", lambda e, k=k, t=t: e.activation(out=hT[:, :, t * 128:(t + 1) * 128], in_=tp[:, k], func=AF.Copy),
                         reads=[B_tp[k]], writes=[B_hT[t]])

                def mm_group(out_ap, pairs, reads, bacc):
                    def fn(e, out_ap=out_ap, pairs=pairs):
                        ins = None
                        n = len(pairs)
                        for i, (l_, r_) in enumerate(pairs):
                            ins = e.matmul(out_ap, lhsT=l_, rhs=r_, start=(i == 0), stop=(i == n - 1))
                        return ins
                    return P.op("pe", fn, reads=reads, writes=[bacc])

                def ffn(l, which):
                    g = load_gain((g_ffn1 if which == 0 else g_ffn2).ap()[l])
                    norm_transpose(g)
                    W = wgu[which]

                    def gu_tile(ti):
                        return ring_tile([(0, 256, wsrc(W, l, 0, 8, 256 * ti, 256 * ti + 256), 8),
                                          (256, 512, wsrc(W, l, 0, 8, DFF + 256 * ti, DFF + 256 * ti + 256), 8)])

                    def gu_work(ti, s, half):
                        for cc in range(2):
                            c = 2 * ti + cc
                            hb = [B_hT[4 * half + q] for q in range(4)]
                            bg = nxt("acc", 6)
                            mm_group(acc[:, bg], [(ring[:, s, j, cc * 128:(cc + 1) * 128], hT[:, j, half * 512:(half + 1) * 512]) for j in range(8)],
                                     hb + [B_ring[s][0]], B_acc[bg])
                            bu = nxt("acc", 6)
                            mm_group(acc[:, bu], [(ring[:, s, j, 256 + cc * 128:256 + (cc + 1) * 128], hT[:, j, half * 512:(half + 1) * 512]) for j in range(8)],
                                     hb + [B_ring[s][1]], B_acc[bu])
                            k = nxt("sg", 2)
                            P.op("act", lambda e, k=k, bg=bg: e.activation(out=sg[:, k], in_=acc[:, bg], func=AF.Silu),
                                 reads=[B_acc[bg]], writes=[B_sg[k]])
                            P.op("dve", lambda e, k=k, bu=bu, c=c, half=half: e.tensor_tensor(
                                out=AT[:, c, half * 512:(half + 1) * 512], in0=acc[:, bu], in1=sg[:, k], op=ALU.mult),
                                reads=[B_acc[bu], B_sg[k]], writes=[B_AT[c][half]])

                    s0 = gu_tile(0)
                    s1 = gu_tile(1)
                    gu_work(0, s0, 0)
                    gu_work(1, s1, 0)
                    gu_work(0, s0, 1)
                    gu_work(1, s1, 1)
                    for ti in range(2, NCH // 2):
                        s = gu_tile(ti)
                        for half in range(2):
                            gu_work(ti, s, half)
                    Wd = wdn[which]
                    for nh in range(2):
                        slots = []
                        for cg in range(3):
                            nj = 8 if cg < 2 else 6
                            slots.append(ring_tile([(0, 512, wsrc(Wd, l, 8 * cg, nj, nh * 512, nh * 512 + 512), nj)]))
                        for t in range(8):
                            b = nxt("acc", 6)
                            mm_group(acc[:, b], [(AT[:, c, t * 128:(t + 1) * 128], ring[:, slots[c // 8], c % 8, :]) for c in range(NCH)],
                                     [B_AT[c][t // 4] for c in range(NCH)] + [B_ring[s_][q] for s_ in slots for q in range(2)], B_acc[b])
                            P.op("dve", lambda e, b=b, t=t, nh=nh: e.scalar_tensor_tensor(
                                out=xs[:, t, nh * 512:(nh + 1) * 512], in0=acc[:, b], scalar=0.5, in1=xs[:, t, nh * 512:(nh + 1) * 512],
                                op0=ALU.mult, op1=ALU.add),
                                reads=[B_acc[b]], writes=[B_xs[t]])

                def outproj(l, tb):
                    for t in range(8):
                        k = t % 2
                        row0 = tb * TB + t * 128
                        P.dma("sp", ocl[:, k], OC.ap()[row0:row0 + 128, :], writes=[B_ocl[k]])
                        transpose_to_hT(ocl, k, B_ocl[k], t)
                    slots = [ring_tile([(0, 512, wsrc(wout, l, 0, 8, nh * 512, nh * 512 + 512), 8)]) for nh in range(2)]
                    for nh in range(2):
                        s = slots[nh]
                        for t in range(8):
                            b = nxt("acc", 6)
                            mm_group(acc[:, b], [(hT[:, j, t * 128:(t + 1) * 128], ring[:, s, j, :]) for j in range(8)],
                                     [B_hT[t], B_ring[s][0], B_ring[s][1]], B_acc[b])
                            P.op("dve", lambda e, b=b, t=t, nh=nh: e.tensor_tensor(
                                out=xs[:, t, nh * 512:(nh + 1) * 512], in0=acc[:, b], in1=xs[:, t, nh * 512:(nh + 1) * 512], op=ALU.add),
                                reads=[B_acc[b]], writes=[B_xs[t]])

                def proj(l, tb):
                    g = load_gain(g_mix.ap()[l])
                    norm_transpose(g)
                    if tb + 1 < NTB:
                        load_xs(tb + 1)
                    P.dma("sp", fbt[:], fbias.ap()[l].partition_broadcast(128), writes=[B_fbt])
                    tok0 = tb * TB

                    def fm_chunk(s, m, half, scale, dsts, bsl):
                        hb = [B_hT[4 * half + q] for q in range(4)]
                        b = nxt("acc", 6)
                        mm_group(acc[:, b], [(ring[:, s, j, m * 128:(m + 1) * 128], hT[:, j, half * 512:(half + 1) * 512]) for j in range(8)],
                                 hb + bsl, B_acc[b])
                        k = nxt("stg", 3)
                        P.op("act", lambda e, b=b, k=k: e.activation(out=stg[:, k], in_=acc[:, b], func=AF.Copy, scale=scale),
                             reads=[B_acc[b]], writes=[B_stg[k]])
                        for (dap, p0, p1, bd) in dsts:
                            P.dma("sp", dap, stg[p0:p1, k], reads=[B_stg[k]])

                    c0 = tok0
                    for which, off in ((0, OFF_FQ), (1, OFF_FK)):
                        s = ring_tile([(0, 512, wsrc([win[0], win[1]], l, 0, 8, off, off + 512), 8)])
                        for half in range(2):
                            for m in range(4):
                                cs = c0 + half * 512
                                if which == 0:
                                    dsts = [(QF.ap()[2 * m, 0:64, cs:cs + 512], 0, 64, None), (QF.ap()[2 * m + 1, 0:64, cs:cs + 512], 64, 128, None)]
                                else:
                                    dsts = [(KF[l][2 * m].ap()[0:64, cs:cs + 512], 0, 64, None), (KF[l][2 * m + 1].ap()[0:64, cs:cs + 512], 64, 128, None)]
                                fm_chunk(s, m, half, 0.125 if which == 0 else 1.0, dsts, [B_ring[s][0], B_ring[s][1]])
                    for which, off, offs in ((0, OFF_DQ, OFF_DQS), (1, OFF_DK, OFF_DKS)):
                        s1 = ring_tile([(0, 512, wsrc([win[0], win[1]], l, 0, 8, off, off + 512), 8)])
                        s2 = ring_tile([(0, 512, wsrc([win[0], win[1]], l, 0, 8, offs, offs + 512), 8)])
                        for half in range(2):
                            cs = c0 + half * 512
                            tg = half
                            if which == 0:
                                P.dma("sp", trig[:, tg, 0], cosT.ap()[:, cs:cs + 512], writes=[B_trig[tg]])
                                P.dma("sp", trig[:, tg, 1], sinT.ap()[:, cs:cs + 512], writes=[B_trig[tg]])
                            hb = [B_hT[4 * half + q] for q in range(4)]
                            for m in range(4):
                                b1 = nxt("acc", 6)
                                mm_group(acc[:, b1], [(ring[:, s1, j, m * 128:(m + 1) * 128], hT[:, j, half * 512:(half + 1) * 512]) for j in range(8)],
                                         hb + B_ring[s1], B_acc[b1])
                                b2 = nxt("acc", 6)
                                mm_group(acc[:, b2], [(ring[:, s2, j, m * 128:(m + 1) * 128], hT[:, j, half * 512:(half + 1) * 512]) for j in range(8)],
                                         hb + B_ring[s2], B_acc[b2])
                                sc = 0.125 if which == 0 else 1.0
                                k1 = nxt("rt", 2)
                                P.op("dve", lambda e, b1=b1, k1=k1, tg=tg, sc=sc: e.scalar_tensor_tensor(
                                    out=rt[:, k1], in0=acc[:, b1], scalar=sc, in1=trig[:, tg, 0], op0=ALU.mult, op1=ALU.mult),
                                     reads=[B_acc[b1], B_trig[tg]], writes=[B_rt[k1]])
                                k = nxt("stg", 3)
                                P.op("dve", lambda e, b2=b2, k=k, k1=k1, tg=tg, sc=sc: e.scalar_tensor_tensor(
                                    out=rt[:, 2], in0=acc[:, b2], scalar=sc, in1=trig[:, tg, 1], op0=ALU.mult, op1=ALU.mult),
                                     reads=[B_acc[b2], B_trig[tg]], writes=[B_rt[2]])
                                P.op("dve", lambda e, k=k, k1=k1: e.tensor_tensor(out=stg[:, k], in0=rt[:, k1], in1=rt[:, 2], op=ALU.add),
                                     reads=[B_rt[k1], B_rt[2]], writes=[B_stg[k]])
                                dap = QD.ap()[m, :, cs:cs + 512] if which == 0 else KD[l][m].ap()[:, cs:cs + 512]
                                P.dma("sp", dap, stg[:, k], reads=[B_stg[k]])
                    sv = ring_tile([(0, 512, wsrc([win[0], win[1]], l, 0, 8, OFF_FV, OFF_FV + 512), 8)])
                    sd = ring_tile([(0, 512, wsrc([win[0], win[1]], l, 0, 8, OFF_DV, OFF_DV + 512), 8)])
                    sf = ring_tile([(0, 8, wsrc([win[0], win[1]], l, 0, 8, OFF_FF, OFF_FF + 8), 8)])
                    for t in range(8):
                        row0 = tok0 + t * 128
                        k = t % 2
                        b = nxt("acc", 6)
                        mm_group(acc[:, b], [(hT[:, j, t * 128:(t + 1) * 128], ring[:, sv, j, :]) for j in range(8)], [B_hT[t]] + B_ring[sv], B_acc[b])
                        P.op("act", lambda e, b=b, k=k: e.activation(out=vft[:, k], in_=acc[:, b], func=AF.Copy),
                             reads=[B_acc[b]], writes=[B_vft[k]])
                        for hh in range(8):
                            P.dma("sp", VF[l][hh].ap()[row0:row0 + 128, :], vft[:, k, hh * 64:(hh + 1) * 64], reads=[B_vft[k]])
                        b = nxt("acc", 6)
                        mm_group(acc[:, b], [(hT[:, j, t * 128:(t + 1) * 128], ring[:, sd, j, :]) for j in range(8)], [B_hT[t]] + B_ring[sd], B_acc[b])
                        P.op("act", lambda e, b=b, k=k: e.activation(out=vdt[:, k], in_=acc[:, b], func=AF.Copy),
                             reads=[B_acc[b]], writes=[B_vdt[k]])
                        for hh in range(4):
                            P.dma("sp", VD[l][hh].ap()[row0:row0 + 128, :], vdt[:, k, hh * 128:(hh + 1) * 128], reads=[B_vdt[k]])
                        b = nxt("acc", 6)
                        mm_group(acc[:, b, 0:8], [(hT[:, j, t * 128:(t + 1) * 128], ring[:, sf, j, 0:8]) for j in range(8)], [B_hT[t], B_ring[sf][0]], B_acc[b])
                        P.op("dve", lambda e, b=b, t=t: e.tensor_tensor(out=zz[:, t], in0=acc[:, b, 0:8], in1=fbt[:], op=ALU.add),
                             reads=[B_acc[b], B_fbt], writes=[B_zz])
                    if 'lsig' in SKIP:
                        return
                    P.op("dve", lambda e: e.tensor_scalar(out=ze[:], in0=zz[:], scalar1=-1.0, scalar2=None, op0=ALU.mult), reads=[B_zz], writes=[B_ze])
                    P.op("dve", lambda e: e.tensor_tensor(out=za[:], in0=zz[:], in1=ze[:], op=ALU.max), reads=[B_zz, B_ze], writes=[B_za])
                    P.op("act", lambda e: e.activation(out=ze[:], in_=za[:], func=AF.Exp, scale=-1.0), reads=[B_za], writes=[B_ze])
                    P.op("act", lambda e: e.activation(out=za[:], in_=ze[:], func=AF.Ln, bias=oneb[:, 0:1], scale=1.0), reads=[B_ze], writes=[B_za])
                    P.op("dve", lambda e: e.scalar_tensor_tensor(out=lf[:], in0=zz[:], scalar=0.0, in1=za[:], op0=ALU.min, op1=ALU.subtract),
                         reads=[B_zz, B_za], writes=[B_lf])
                    if 'cumsum' in SKIP:
                        return
                    for t in range(8):
                        b = nxt("acc", 6)
                        P.op("pe", lambda e, b=b, t=t: e.matmul(acc[0:8, b, 0:128], lhsT=lf[:, t], rhs=triu[:], start=True, stop=True),
                             reads=[B_lf, B_triu], writes=[B_acc[b]])
                        P.op("dve", lambda e, b=b, t=t: e.tensor_copy(out=wT[:, tok0 + t * 128: tok0 + (t + 1) * 128], in_=acc[0:8, b, 0:128]),
                             reads=[B_acc[b]], writes=[B_wT])

                def final_norm(tb):
                    g = load_gain(g_fin.ap()[0])
                    for t in range(8):
                        P.op("act", lambda e, t=t: e.activation(out=junk[:], in_=xs[:, t], func=AF.Square, accum_out=ssq[:, t:t + 1]),
                             reads=[B_xs[t]], writes=[B_junk, B_ssq], noembed=True)
                    P.op("act", lambda e: e.activation(out=rms[:], in_=ssq[:], func=AF.Sqrt, bias=epsb[:, 0:1], scale=1.0 / D),
                         reads=[B_ssq], writes=[B_rms])
                    P.op("dve", lambda e: e.reciprocal(out=rstd[:], in_=rms[:]), reads=[B_rms], writes=[B_rstd])
                    for t in range(8):
                        k = t % 2
                        P.op("dve", lambda e, t=t, k=k: e.scalar_tensor_tensor(
                            out=yt[:, k], in0=xs[:, t], scalar=rstd[:, t:t + 1], in1=gain[:, g], op0=ALU.mult, op1=ALU.mult),
                            reads=[B_xs[t], B_rstd, B_gain[g]], writes=[B_yt[k]])
                        row0 = tb * TB + t * 128
                        P.dma("sp", y_out.ap()[row0:row0 + 128, :], yt[:, k], reads=[B_yt[k]])

                def load_xs(tb):
                    src = x_in if first else X1
                    for t in range(8):
                        row0 = tb * TB + t * 128
                        P.dma("sp", xs[:, t], src.ap()[row0:row0 + 128, :], writes=[B_xs[t]])

                for tb in range(NTB):
                    if tb == 0 or last:
                        load_xs(tb)
                    if not first:
                        outproj(layer - 1, tb)
                        ffn(layer - 1, 1)
                    if last:
                        final_norm(tb)
                    else:
                        ffn(layer, 0)
                        for t in range(8):
                            row0 = tb * TB + t * 128
                            P.dma("sp", X1.ap()[row0:row0 + 128, :], xs[:, t], reads=[B_xs[t]])
                        if 'proj' not in SKIP:
                            proj(layer, tb)
                P.flush()

        def exchange(l):
            with ExitStack() as ps:
                sb = lambda name, shape, dtp: ps.enter_context(nc.sbuf_tensor(uniq(name), shape, dtp))
                tt = sb("tt", [8, NT], F32)
                tta = sb("tta", [8, 4, NT], F32)
                ttg = sb("ttg", [8, NT, 4], F32)
                onesf = sb("onesf", [8, 128], F32)
                pinc = sb("pinc", [8, NT, 4], F32)
                pex = sb("pex", [8, NT, 4], F32)
                oh = sb("oh", [8, 4], F32)
                pown = sb("pown", [8, NT], F32)
                cT = sb("cT", [8, NLOC], F32)
                r1 = sb("r1", [8, NLOC], F32)
                pq = sb("pq", [8, 3, NLOC], BF16)
                pk = sb("pk", [8, 3, NLOC], BF16)
                Bt = {k: Buf() for k in ("tt", "tta", "ttg", "onesf", "pinc", "pex", "oh", "pown", "cT", "r1", "pq", "pk", "TTI", "TTg")}
                P.op("dve", lambda e: e.tensor_copy(out=tt[:], in_=wT[:].rearrange("h (g t) -> h g t", t=128)[:, :, 127]),
                     reads=[B_wT], writes=[Bt["tt"]])
                t_tti = P.dma("sp", TTI[l].ap()[:, :], tt[:], reads=[Bt["tt"]], writes=[Bt["TTI"]])
                c_kd = [None] * 4
                c_vd = [None] * 4
                c_kd[0] = collective(KD[l][0], KDg[l][0], [])
                c_vd[0] = collective(VD[l][0], VDg[l][0], [])
                c_tt = collective(TTI[l], TTg[l], [t_tti])
                for h in range(1, 4):
                    c_kd[h] = collective(KD[l][h], KDg[l][h], [])
                    c_vd[h] = collective(VD[l][h], VDg[l][h], [])
                c_vf = [collective(VF[l][h], VFg[l][h], []) for h in range(8)]
                P.dma("sp", tta[:], TTg[l].ap().rearrange("r h g -> h r g"), writes=[Bt["tta"]], extra=[c_tt])
                P.dma("sp", oh[:], onehot.ap()[:, :], writes=[Bt["oh"]])
                P.op("dve", lambda e: e.memset(onesf[:], 1.0), writes=[Bt["onesf"]])
                P.op("dve", lambda e: e.tensor_copy(out=ttg[:], in_=tta[:].rearrange("h r g -> h g r")), reads=[Bt["tta"]], writes=[Bt["ttg"]])
                P.op("dve", lambda e: e.tensor_tensor_scan(out=pinc[:].rearrange("h g r -> h (g r)"), data0=onesf[:],
                                                           data1=ttg[:].rearrange("h g r -> h (g r)"), initial=0.0, op0=ALU.mult, op1=ALU.add),
                     reads=[Bt["ttg"], Bt["onesf"]], writes=[Bt["pinc"]])
                P.op("dve", lambda e: e.tensor_tensor(out=pex[:], in0=pinc[:], in1=ttg[:], op=ALU.subtract), reads=[Bt["pinc"], Bt["ttg"]], writes=[Bt["pex"]])
                P.op("dve", lambda e: e.tensor_tensor(out=pinc[:], in0=pex[:], in1=oh[:].unsqueeze(1).to_broadcast([8, NT, 4]), op=ALU.mult),
                     reads=[Bt["pex"], Bt["oh"]], writes=[Bt["pinc"]])
                P.op("dve", lambda e: e.tensor_reduce(out=pown[:], in_=pinc[:], axis=AX.X, op=ALU.add), reads=[Bt["pinc"]], writes=[Bt["pown"]])
                P.op("dve", lambda e: e.tensor_tensor(out=cT[:].rearrange("h (g t) -> h g t", t=128), in0=wT[:].rearrange("h (g t) -> h g t", t=128),
                                                      in1=pown[:].unsqueeze(2).to_broadcast([8, NT, 128]), op=ALU.add),
                     reads=[B_wT, Bt["pown"]], writes=[Bt["cT"]])
                P.op("dve", lambda e: e.tensor_copy(out=pq[:, 0], in_=cT[:]), reads=[Bt["cT"]], writes=[Bt["pq"]])
                P.op("dve", lambda e: e.tensor_tensor(out=r1[:], in0=cT[:], in1=pq[:, 0], op=ALU.subtract), reads=[Bt["cT"], Bt["pq"]], writes=[Bt["r1"]])
                P.op("dve", lambda e: e.tensor_copy(out=pq[:, 1], in_=r1[:]), reads=[Bt["r1"]], writes=[Bt["pq"]])
                P.op("dve", lambda e: e.tensor_tensor(out=cT[:], in0=r1[:], in1=pq[:, 1], op=ALU.subtract), reads=[Bt["r1"], Bt["pq"]], writes=[Bt["cT"]])
                P.op("dve", lambda e: e.tensor_copy(out=pq[:, 2], in_=cT[:]), reads=[Bt["cT"]], writes=[Bt["pq"]])
                P.op("dve", lambda e: e.tensor_scalar(out=pk[:], in0=pq[:], scalar1=-1.0, scalar2=None, op0=ALU.mult), reads=[Bt["pq"]], writes=[Bt["pk"]])
                P.dma("sp", QF.ap()[:, 64:67, :], pq[:], reads=[Bt["pq"]])
                c_kf = [None] * 8
                for h in range(8):
                    t_pk = P.dma("sp", KF[l][h].ap()[67:70, :].unsqueeze(0), pk[h:h + 1, :, :], reads=[Bt["pk"]])
                    c_kf[h] = collective(KF[l][h], KFg[l][h], [t_pk])
                P.flush()
            return {"kd": c_kd, "vd": c_vd, "vf": c_vf, "kf": c_kf}

        def attention(l, cc):
            lam_init = 0.8 - 0.6 * math.exp(-0.3 * l)
            with ExitStack() as ps:
                sb = lambda name, shape, dtp: ps.enter_context(nc.sbuf_tensor(uniq(name), shape, dtp))
                Ksb = sb("Ksb", [128, 2, 4, NLOC], BF16)
                Vsb = sb("Vsb", [128, 2, 4 * NT * 129], BF16)
                Qsb = sb("Qsb", [128, 2, NLOC], BF16)
                PT = sb("PT", [128, 3, 4, 512], BF16)
                Sm = sb("Sm", [128, 2, 4, 512], F32)
                mask = sb("mask", [128, 512], F32)
                lamt = sb("lamt", [128, 4, 64], F32)
                lprod = sb("lprod", [128, 2, 64], F32)
                lsum = sb("lsum", [128, 2], F32)
                lexp = sb("lexp", [128, 2], F32)
                neglam = sb("neglam", [128, 1], F32)
                gsub = sb("gsub", [128, 128], F32)
                nhalf = sb("nhalf", [128, 4], F32)
                rl = sb("rl", [128, 2, 4, 2], F32)
                ocf = sb("ocf", [128, 2, 4, 64], BF16)
                a1 = sb("a1", [128, 4, 128], F32)
                t2 = sb("t2", [128, 4, 128], F32)
                dd = sb("dd", [128, 4, 128], F32)
                ssqd = sb("ssqd", [128, 4], F32)
                rstdd = sb("rstdd", [128, 4], F32)
                ocd = sb("ocd", [128, 2, 4, 128], BF16)
                PS = ps.enter_context(nc.psum_tensor(uniq("PS"), [128, 8, 512], F32))
                B_bank = [Buf() for _ in range(8)]
                B_K = [[Buf() for _ in range(4)] for _ in range(2)]
                B_V = [[Buf() for _ in range(8)] for _ in range(2)]
                B_Q = [Buf(), Buf()]
                B_PT = [Buf() for _ in range(3)]
                B_Sm = [Buf(), Buf()]
                B_S = [Buf(), Buf()]
                B_O = Buf()
                Bm = {k: Buf() for k in ("mask", "lamt", "lprod", "lsum", "lexp", "neglam", "gsub", "nhalf", "rl0", "rl1", "ocf0", "ocf1",
                                         "a1", "t2", "dd", "ssqd", "rstdd", "ocd0", "ocd1")}
                P.dma("sp", mask[:], maskd.ap()[:, :], writes=[Bm["mask"]])
                P.dma("sp", lamt[:].rearrange("p a d -> p (a d)"), lam_in.ap()[l].rearrange("a d -> (a d)").partition_broadcast(128), writes=[Bm["lamt"]])
                P.dma("sp", gsub[:], subln.ap()[l].partition_broadcast(128), writes=[Bm["gsub"]])
                P.op("dve", lambda e: e.tensor_tensor(out=lprod[:], in0=lamt[:, 0:4:2, :], in1=lamt[:, 1:4:2, :], op=ALU.mult),
                     reads=[Bm["lamt"]], writes=[Bm["lprod"]])
                P.op("dve", lambda e: e.tensor_reduce(out=lsum[:], in_=lprod[:], axis=AX.X, op=ALU.add), reads=[Bm["lprod"]], writes=[Bm["lsum"]])
                P.op("act", lambda e: e.activation(out=lexp[:], in_=lsum[:], func=AF.Exp), reads=[Bm["lsum"]], writes=[Bm["lexp"]])
                P.op("dve", lambda e: e.tensor_tensor(out=neglam[:], in0=lexp[:, 1:2], in1=lexp[:, 0:1], op=ALU.subtract), reads=[Bm["lexp"]], writes=[Bm["neglam"]])
                P.op("dve", lambda e: e.tensor_scalar(out=neglam[:], in0=neglam[:], scalar1=-lam_init, scalar2=None, op0=ALU.add), reads=[], writes=[Bm["neglam"]])
                P.op("dve", lambda e: e.tensor_scalar(out=gsub[:], in0=gsub[:], scalar1=1.0 - lam_init, scalar2=None, op0=ALU.mult), reads=[], writes=[Bm["gsub"]])
                P.op("dve", lambda e: e.memset(nhalf[:], -0.5), writes=[Bm["nhalf"]])

                units = [("d", h) for h in range(4)] + [("f", h) for h in range(8)]

                vkind = [None, None]
                kkind = [None, None]

                def load_unit(ui):
                    kind, h = units[ui]
                    k = ui % 2
                    dv = 65 if kind == "f" else 129
                    if vkind[k] != kind:
                        vkind[k] = kind
                        P.op("dve", lambda e, k=k, dv=dv: e.memset(Vsb[:, k, 0:4 * NT * dv].rearrange("p (n d) -> p n d", d=dv)[:, :, dv - 1:dv], 1.0),
                             writes=B_V[k])
                    if kind == "f" and kkind[k] != "f":
                        kkind[k] = "f"
                        P.op("pool", lambda e, k=k: e.memset(Ksb[64:128, k], 0.0), writes=B_K[k])
                        P.op("pool", lambda e, k=k: e.memset(Qsb[64:128, k], 0.0), writes=[B_Q[k]])
                    if kind == "f":
                        for r in range(4):
                            P.dma("sp", Ksb[0:70, k, r, :], KFg[l][h].ap()[r], writes=[B_K[k][r]], extra=[cc["kf"][h]])
                        P.dma("sp", Qsb[0:70, k, :], QF.ap()[h, :, :], writes=[B_Q[k]])
                        for r in range(4):
                            src = VFg[l][h].ap()[r].rearrange("(g p) d -> p g d", p=128)
                            for hf in range(2):
                                P.dma("sp", Vsb[:, k, (r * NT + 16 * hf) * 65:(r * NT + 16 * hf + 16) * 65].rearrange("p (g d) -> p g d", d=65)[:, :, 0:64],
                                      src[:, 16 * hf:16 * hf + 16, :], writes=[B_V[k][2 * r + hf]], extra=[cc["vf"][h]])
                    else:
                        for r in range(4):
                            P.dma("sp", Ksb[:, k, r, :], KDg[l][h].ap()[r], writes=[B_K[k][r]], extra=[cc["kd"][h]])
                        P.dma("sp", Qsb[:, k, :], QD.ap()[h, :, :], writes=[B_Q[k]])
                        for r in range(4):
                            src = VDg[l][h].ap()[r].rearrange("(g p) d -> p g d", p=128)
                            for hf in range(2):
                                P.dma("sp", Vsb[:, k, (r * NT + 16 * hf) * 129:(r * NT + 16 * hf + 16) * 129].rearrange("p (g d) -> p g d", d=129)[:, :, 0:128],
                                      src[:, 16 * hf:16 * hf + 16, :], writes=[B_V[k][2 * r + hf]], extra=[cc["vd"][h]])

                cnt = {"S": 0, "PT": 0, "Sm": 0, "oc": 0}

                def run_unit(ui):
                    kind, h = units[ui]
                    kb = ui % 2
                    dv = 65 if kind == "f" else 129
                    KR = 128 if kind == "f" else 64
                    if kind == "f":
                        sbufs = [(0, 4), (4, 3)]
                        obanks = [7]
                    else:
                        sbufs = [(0, 2), (2, 2)]
                        obanks = [4, 5, 6, 7]
                    B_Ob = [B_bank[o] for o in obanks]

                    def Kap(c, r, g):
                        p0 = 0 if kind == "f" else 64 * c
                        return Ksb[p0:p0 + KR, kb, r, g * 128:(g + 1) * 128]

                    def Qap(c, c0, c1):
                        p0 = 0 if kind == "f" else 64 * c
                        return Qsb[p0:p0 + KR, kb, c0:c1]

                    def Vap(r, g):
                        return Vsb[:, kb, (r * NT + g) * dv:(r * NT + g + 1) * dv]

                    def Oap(a, c):
                        if kind == "f":
                            return PS[:, 7, 128 * a:128 * a + 65]
                        return PS[:, 4 + a, 256 * c:256 * c + 129]

                    ncomp = 1 if kind == "f" else 2
                    for J in range(8):
                        segs = []
                        q0 = 512 * J
                        for g in range(4 * J + 3):
                            b = g - 4 * J
                            a_lo = 0 if b < 0 else b + 1
                            N = 128 * (4 - a_lo)
                            qa, qb = q0 + 128 * a_lo, q0 + 512
                            for r in range(4):
                                for c in range(ncomp):
                                    segs.append((N, False, [(0, N, Kap(c, r, g), Qap(c, qa, qb))],
                                                 [((a - a_lo) * 128, Vap(r, g), Oap(a, c)) for a in range(a_lo, 4)]))
                        for b in range(4):
                            g = 4 * J + b
                            for c in range(ncomp):
                                segs.append((512, True,
                                             [(r * 128, 128, Kap(c, r, g), Qap(c, q0 + 128 * b, q0 + 128 * b + 128)) for r in range(4)],
                                             [(r * 128, Vap(r, g), Oap(b, c)) for r in range(4)]))
                        tiles = []
                        sidx = cnt["S"]
                        i = 0
                        while i < len(segs):
                            cap = sbufs[sidx % 2][1]
                            j = i + 1
                            while j < len(segs) and j - i < cap and segs[j][0] == segs[i][0] and segs[j][1] == segs[i][1]:
                                j += 1
                            tiles.append(segs[i:j])
                            sidx += 1
                            i = j

                        nt_ = len(tiles)
                        info = [None] * nt_

                        def emit_qk(n):
                            tsegs = tiles[n]
                            ns = len(tsegs)
                            N, masked = tsegs[0][0], tsegs[0][1]
                            bank0 = sbufs[cnt["S"] % 2][0]
                            cnt["S"] += 1
                            pti = cnt["PT"] % 3
                            cnt["PT"] += 1
                            bsb = [B_bank[bank0 + si] for si in range(ns)]

                            def fqk(e, tsegs=tsegs, bank0=bank0):
                                ins = None
                                for si, sg_ in enumerate(tsegs):
                                    for (c0, w, lap, rap) in sg_[2]:
                                        ins = e.matmul(PS[:, bank0 + si, c0:c0 + w], lhsT=lap, rhs=rap, start=True, stop=True)
                                return ins
                            P.op("pe", fqk, reads=B_K[kb] + [B_Q[kb]], writes=bsb)
                            if masked:
                                smi = cnt["Sm"] % 2
                                cnt["Sm"] += 1
                                P.op("dve", lambda e, bank0=bank0, ns=ns, smi=smi: e.tensor_tensor(
                                    out=Sm[:, smi, 0:ns], in0=PS[:, bank0:bank0 + ns, :], in1=mask[:].unsqueeze(1).to_broadcast([128, ns, 512]), op=ALU.add),
                                    reads=bsb + [Bm["mask"]], writes=[B_Sm[smi]])
                                P.op("act", lambda e, smi=smi, pti=pti, ns=ns: e.activation(out=PT[:, pti, 0:ns], in_=Sm[:, smi, 0:ns], func=AF.Exp),
                                     reads=[B_Sm[smi]], writes=[B_PT[pti]])
                            else:
                                P.op("act", lambda e, bank0=bank0, ns=ns, pti=pti, N=N: e.activation(
                                    out=PT[:, pti, 0:ns, 0:N], in_=PS[:, bank0:bank0 + ns, 0:N], func=AF.Exp),
                                     reads=bsb, writes=[B_PT[pti]])
                            info[n] = pti

                        def emit_pv(n, first, lastt):
                            tsegs = tiles[n]
                            pti = info[n]

                            def fpv(e, tsegs=tsegs, pti=pti, first=first, lastt=lastt):
                                ins = None
                                if first:
                                    if kind == "f":
                                        e.matmul(PS[:, 7, 0:449], lhsT=zer_bf[:, 0:128], rhs=zer_bf[:, 0:449], start=True, stop=False, skip_group_check=True)
                                    else:
                                        for a in range(4):
                                            e.matmul(PS[:, 4 + a, 0:385], lhsT=zer_bf[:, 0:128], rhs=zer_bf[:, 0:385], start=True, stop=False, skip_group_check=True)
                                for si, sg_ in enumerate(tsegs):
                                    for (c0, vap, oap) in sg_[3]:
                                        ins = e.matmul(oap, lhsT=PT[:, pti, si, c0:c0 + 128], rhs=vap, start=False, stop=lastt, skip_group_check=True)
                                return ins
                            P.op("pe", fpv, reads=[B_PT[pti], B_zer] + B_V[kb], writes=B_Ob)

                        for n in range(nt_ + 1):
                            if n < nt_:
                                emit_qk(n)
                            if n >= 1:
                                emit_pv(n - 1, n - 1 == 0, n - 1 == nt_ - 1)

                        oi = cnt["oc"] % 2
                        cnt["oc"] += 1
                        rows = OC.ap()[q0:q0 + 512, :].rearrange("(a p) d -> p a d", p=128)
                        if kind == "f":
                            O7 = PS[:, 7, :].rearrange("p (a w) -> p a w", w=128)
                            P.op("dve", lambda e, oi=oi, O7=O7: e.reciprocal(out=rl[:, oi, :, 0:1], in_=O7[:, :, 64:65]), reads=B_Ob, writes=[Bm[f"rl{oi}"]])
                            P.op("dve", lambda e, oi=oi, O7=O7: e.tensor_tensor(out=ocf[:, oi], in0=O7[:, :, 0:64], in1=rl[:, oi, :, 0:1].to_broadcast([128, 4, 64]), op=ALU.mult),
                                 reads=B_Ob + [Bm[f"rl{oi}"]], writes=[Bm[f"ocf{oi}"]])
                            P.dma("sp", rows[:, :, h * 64:(h + 1) * 64], ocf[:, oi], reads=[Bm[f"ocf{oi}"]])
                        else:
                            O4 = PS[:, 4:8, :].rearrange("p a (c w) -> p a c w", c=2)
                            P.op("dve", lambda e, oi=oi, O4=O4: e.reciprocal(out=rl[:, oi].unsqueeze(3), in_=O4[:, :, :, 128:129]), reads=B_Ob, writes=[Bm[f"rl{oi}"]])
                            P.op("dve", lambda e, oi=oi, O4=O4: e.tensor_tensor(out=a1[:], in0=O4[:, :, 0, 0:128], in1=rl[:, oi, :, 0:1].to_broadcast([128, 4, 128]), op=ALU.mult),
                                 reads=B_Ob + [Bm[f"rl{oi}"]], writes=[Bm["a1"]])
                            P.op("dve", lambda e, oi=oi: e.tensor_scalar(out=rl[:, oi, :, 1:2], in0=rl[:, oi, :, 1:2], scalar1=neglam[:, 0:1], scalar2=None, op0=ALU.mult),
                                 reads=[Bm["neglam"]], writes=[Bm[f"rl{oi}"]])
                            P.op("dve", lambda e, oi=oi, O4=O4: e.tensor_tensor(out=t2[:], in0=O4[:, :, 1, 0:128], in1=rl[:, oi, :, 1:2].to_broadcast([128, 4, 128]), op=ALU.mult),
                                 reads=B_Ob + [Bm[f"rl{oi}"]], writes=[Bm["t2"]])
                            P.op("dve", lambda e: e.tensor_tensor(out=dd[:], in0=a1[:], in1=t2[:], op=ALU.add), reads=[Bm["a1"], Bm["t2"]], writes=[Bm["dd"]])
                            P.op("dve", lambda e: e.tensor_tensor(out=a1[:], in0=dd[:], in1=dd[:], op=ALU.mult), reads=[Bm["dd"]], writes=[Bm["a1"]])
                            P.op("dve", lambda e: e.tensor_reduce(out=ssqd[:], in_=a1[:], axis=AX.X, op=ALU.add), reads=[Bm["a1"]], writes=[Bm["ssqd"]])
                            P.op("dve", lambda e: e.tensor_scalar(out=ssqd[:], in0=ssqd[:], scalar1=1.0 / 128, scalar2=EPS, op0=ALU.mult, op1=ALU.add),
                                 reads=[], writes=[Bm["ssqd"]])
                            P.op("pool", lambda e: e.tensor_tensor(out=rstdd[:], in0=ssqd[:], in1=nhalf[:], op=ALU.pow), reads=[Bm["ssqd"], Bm["nhalf"]], writes=[Bm["rstdd"]])
                            P.op("dve", lambda e: e.tensor_tensor(out=t2[:], in0=dd[:], in1=rstdd[:].unsqueeze(2).to_broadcast([128, 4, 128]), op=ALU.mult),
                                 reads=[Bm["dd"], Bm["rstdd"]], writes=[Bm["t2"]])
                            P.op("dve", lambda e, oi=oi: e.tensor_tensor(out=ocd[:, oi], in0=t2[:], in1=gsub[:].unsqueeze(1).to_broadcast([128, 4, 128]), op=ALU.mult),
                                 reads=[Bm["t2"], Bm["gsub"]], writes=[Bm[f"ocd{oi}"]])
                            P.dma("sp", rows[:, :, 512 + h * 128:512 + (h + 1) * 128], ocd[:, oi], reads=[Bm[f"ocd{oi}"]])

                load_unit(0)
                for ui in range(len(units)):
                    if ui + 1 < len(units):
                        load_unit(ui + 1)
                    run_unit(ui)
                P.flush()

        def run_all():
            n = 0
            phase_tokens(0, True, False); n += 1
            if n >= stop: return
            cc = exchange(0); n += 1
            if n >= stop: return
            attention(0, cc); n += 1
            if n >= stop: return
            phase_tokens(1, False, False); n += 1
            if n >= stop: return
            cc = exchange(1); n += 1
            if n >= stop: return
            attention(1, cc); n += 1
            if n >= stop: return
            phase_tokens(2, False, True)
        run_all()
        if debug:
            P.dma("sp", dbg["X1"].ap(), X1.ap())
            if 'proj' not in SKIP:
                if stop < 2:
                    P.dma("sp", dbg["QF"].ap()[:, 0:64, :], QF.ap()[:, 0:64, :])
                else:
                    P.dma("sp", dbg["QF"].ap(), QF.ap())
                P.dma("sp", dbg["QD"].ap(), QD.ap())
                for h in range(8):
                    if stop < 2:
                        P.dma("sp", dbg["KF"].ap()[h, 0:64, :], KF[0][h].ap()[0:64, :])
                    else:
                        P.dma("sp", dbg["KF"].ap()[h], KF[0][h].ap())
                    P.dma("sp", dbg["VF"].ap()[h], VF[0][h].ap())
                for h in range(4):
                    P.dma("sp", dbg["KD"].ap()[h], KD[0][h].ap())
                    P.dma("sp", dbg["VD"].ap()[h], VD[0][h].ap())
                if stop >= 2:
                    for r in range(4):
                        P.dma("sp", dbg["KFg"].ap()[r], KFg[0][3].ap()[r], extra=[(ccsem, ccsem.cnt)])
                        P.dma("sp", dbg["VDg"].ap()[r], VDg[0][2].ap()[r], extra=[(ccsem, ccsem.cnt)])
                if stop >= 3:
                    P.dma("sp", dbg["OC"].ap(), OC.ap())
            P.flush()
    return nc


def _rope_tables(r):
    inv_freq = (1.0 / (10000.0 ** (np.arange(0, 64, 2, dtype=np.float32) / np.float32(64)))).astype(np.float32)
    i = np.arange(NT)[:, None]
    p = np.arange(128)[None, :]
    pos = ((4 * i + r) * 128 + p).reshape(-1).astype(np.float32)
    freqs = (pos[:, None] * inv_freq[None, :]).astype(np.float32)
    cos = np.cos(freqs).astype(np.float32)
    sin = np.sin(freqs).astype(np.float32)
    d = np.arange(128) % 64
    fi = d % 32
    sign = np.where(d < 32, -1.0, 1.0).astype(np.float32)
    cosT = np.ascontiguousarray(cos[:, fi].T)
    sinT = np.ascontiguousarray((sin[:, fi] * sign[None, :]).T)
    return cosT, sinT


def _swap_cols(w):
    s = w.reshape(w.shape[:-1] + (8, 2, 32))[..., ::-1, :]
    return s.reshape(w.shape)


_NC_CACHE = {}


def kernel(x, ffn1_norm, ffn1_w_gate_up, ffn1_w_down, mix_norm, w_in, forget_bias,
           lambda_q1, lambda_k1, lambda_q2, lambda_k2, diff_subln, w_out,
           ffn2_norm, ffn2_w_gate_up, ffn2_w_down, final_norm, _debug=False, _stop=99):
    f32 = lambda a: np.ascontiguousarray(np.asarray(a, dtype=np.float32))
    x = f32(x)
    w_in = f32(w_in)
    win_ext = np.ascontiguousarray(np.concatenate(
        [w_in, _swap_cols(w_in[..., OFF_DQ:OFF_DQ + 512]), _swap_cols(w_in[..., OFF_DK:OFF_DK + 512])], axis=-1))
    lam_in = np.ascontiguousarray(np.stack([f32(lambda_q1), f32(lambda_k1), f32(lambda_q2), f32(lambda_k2)], axis=1))
    common = {
        "wgu1": f32(ffn1_w_gate_up), "wgu2": f32(ffn2_w_gate_up), "wd1": f32(ffn1_w_down), "wd2": f32(ffn2_w_down),
        "win": win_ext, "wout": f32(w_out), "g_ffn1": f32(ffn1_norm), "g_mix": f32(mix_norm), "g_ffn2": f32(ffn2_norm),
        "g_fin": f32(final_norm).reshape(1, D), "fbias": f32(forget_bias), "lam_in": lam_in, "subln": f32(diff_subln),
        "identd": np.eye(128, dtype=np.float32).astype(ml_dtypes.bfloat16),
        "triud": np.triu(np.ones((128, 128), dtype=np.float32)),
    }
    in_maps = []
    for c in range(8):
        b, r = c // 4, c % 4
        xl = np.ascontiguousarray(x[b].reshape(NT, 4, 128, D)[:, r].reshape(NLOC, D))
        cosT, sinT = _rope_tables(r)
        kk = np.arange(128)[:, None]
        qq = np.arange(128)[None, :]
        m = np.zeros((128, 4, 128), dtype=np.float32)
        for rp in range(4):
            if rp > r:
                m[:, rp, :] = NEG
            elif rp == r:
                m[:, rp, :] = np.where(kk <= qq, 0.0, NEG)
        oh = np.zeros((8, 4), dtype=np.float32)
        oh[:, r] = 1.0
        d = dict(common)
        d.update({"x": xl, "cosT": cosT, "sinT": sinT, "maskd": m.reshape(128, 512), "onehot": oh})
        in_maps.append(d)
    if _stop == -1:
        return in_maps
    key = (bool(_debug), _stop, tuple(sorted(SKIP)))
    if key not in _NC_CACHE:
        _NC_CACHE[key] = build(debug=_debug, stop=_stop)
    nc = _NC_CACHE[key]
    res = run_bass_kernel_spmd(nc, in_maps, core_ids=list(range(8)))
    out = np.empty((2, NT * 4 * 128, D), dtype=np.float32)
    ov = out.reshape(2, NT, 4, 128, D)
    for c in range(8):
        b, r = c // 4, c % 4
        ov[b, :, r] = np.asarray(res.results[c]["y"], dtype=np.float32).reshape(NT, 128, D)
    if _debug:
        return out, res
    return out
```
